# Optimizing a Trainium2 kernel written in Bass

```python
import jax
import jax.numpy as jnp
from jax import lax
import numpy as np

D_MODEL = 2048
BATCH = 4
SEQ = 8192
DEPTH = 4
DEC_BATCH = 4
DEC_SEQ = 2048
PAST_LEN = 128

N_EVEN = (DEPTH + 1) // 2
N_ODD = DEPTH // 2
N_SUB = 3
EPS = 1e-6
RES_HALF = 0.5

FFN_DIM = 5632

HG_HEADS = 8
HG_DK = 128
HG_DV = 128
HG_WIDTH = HG_HEADS * HG_DK

MLA_HEADS = 8
MLA_NOPE = 128
MLA_ROPE = 64
MLA_QK = MLA_NOPE + MLA_ROPE
MLA_V = 128
Q_LORA = 512
KV_LORA = 512
ROPE_THETA = 10000.0
Q_BLOCK = 128

GLA_HEADS = 4
GLA_DK = 256
GLA_DV = 512
GLA_GATE_RANK = 16
GLA_GATE_NORM = 16.0

CHUNK = 64

EV_IN = 5 * HG_WIDTH + Q_LORA + KV_LORA + MLA_ROPE
EV_MIX = HG_HEADS * HG_DV + MLA_HEADS * MLA_V
OD_IN = 2 * GLA_HEADS * GLA_DK + 2 * GLA_HEADS * GLA_DV + 2 * GLA_GATE_RANK
OD_MIX = GLA_HEADS * GLA_DV

kernel_name = 'hybrid_bidir_hgrn2_mla_gla_encoder'


def rms_norm(x, g):
    xf = x.astype(jnp.float32)
    y = xf * lax.rsqrt(jnp.mean(xf * xf, axis=-1, keepdims=True) + EPS)
    return (y * g.astype(jnp.float32)).astype(x.dtype)


def swiglu(h, w13, w2):
    g, u = jnp.split(h @ w13, 2, axis=-1)
    return (jax.nn.silu(g) * u) @ w2


def split_heads(a, n_heads):
    B, T, _ = a.shape
    return a.reshape(B, T, n_heads, -1).transpose(0, 2, 1, 3)


def merge_heads(a):
    B, H, T, d = a.shape
    return a.transpose(0, 2, 1, 3).reshape(B, T, H * d)


def apply_rope(x, pos):
    half = MLA_ROPE // 2
    inv_freq = ROPE_THETA ** (-jnp.arange(half, dtype=jnp.float32) / half)
    ang = pos.astype(jnp.float32)[:, None] * inv_freq[None, :]
    cos, sin = jnp.cos(ang), jnp.sin(ang)
    xf = x.astype(jnp.float32)
    x1, x2 = xf[..., :half], xf[..., half:]
    return jnp.concatenate([x1 * cos - x2 * sin, x1 * sin + x2 * cos], axis=-1).astype(x.dtype)


def gated_scan(q, k, v, log_f):
    B, H, T, dk = q.shape
    dv = v.shape[-1]
    n = T // CHUNK

    def chunks(a):
        return a.reshape(B, H, n, CHUNK, a.shape[-1]).transpose(2, 0, 1, 3, 4)

    tri = jnp.tril(jnp.ones((CHUNK, CHUNK), dtype=bool))[:, :, None]

    def step(S, inp):
        qc, kc, vc, gc = inp
        b = jnp.cumsum(gc, axis=2)
        diff = jnp.where(tri, b[:, :, :, None, :] - b[:, :, None, :, :], -jnp.inf)
        att = jnp.einsum('bhid,bhjd,bhijd->bhij', qc, kc, jnp.exp(diff))
        o = jnp.einsum('bhij,bhje->bhie', att, vc) + jnp.einsum('bhid,bhde->bhie', qc * jnp.exp(b), S)
        b_last = b[:, :, -1:, :]
        S = jnp.exp(b_last[:, :, 0, :, None]) * S + jnp.einsum('bhjd,bhje->bhde', kc * jnp.exp(b_last - b), vc)
        return S, o

    S0 = jnp.zeros((B, H, dk, dv), jnp.float32)
    _, o = lax.scan(step, S0, (chunks(q), chunks(k), chunks(v), chunks(log_f)))
    return o.transpose(1, 2, 0, 3, 4).reshape(B, H, T, dv)


def bidirectional_scan(q, k_fwd, k_bwd, v, lf_fwd, lf_bwd):
    rev = lambda a: jnp.flip(a, axis=2)
    fwd = gated_scan(q, k_fwd, v, lf_fwd)
    bwd = rev(gated_scan(rev(q), rev(k_bwd), rev(v), rev(lf_bwd)))
    return fwd + bwd


def hgrn2_mixer(u, lb_fwd, lb_bwd, onorm_g):
    f32 = jnp.float32
    q, z_fwd, z_bwd, i, g = jnp.split(u, 5, axis=-1)
    q = jax.nn.silu(split_heads(q, HG_HEADS).astype(f32))
    v = split_heads(i, HG_HEADS).astype(f32)

    def forget(z, lb):
        z = split_heads(z, HG_HEADS).astype(f32)
        lb = lb.reshape(HG_HEADS, 1, HG_DK)
        log_f = jnp.logaddexp(jnp.log(lb), jnp.log1p(-lb) + jax.nn.log_sigmoid(z))
        k = (1.0 - lb) * jax.nn.sigmoid(-z)
        return k, log_f

    k_fwd, lf_fwd = forget(z_fwd, lb_fwd)
    k_bwd, lf_bwd = forget(z_bwd, lb_bwd)
    o = bidirectional_scan(q, k_fwd, k_bwd, v, lf_fwd, lf_bwd)
    o = rms_norm(o, onorm_g) * jax.nn.silu(split_heads(g, HG_HEADS).astype(f32))
    return merge_heads(o).astype(u.dtype)


def block_attention(q, k, v):
    B, H, T, dq = q.shape
    nb = T // Q_BLOCK
    scale = dq ** -0.5
    qb = q.reshape(B, H, nb, Q_BLOCK, dq).transpose(2, 0, 1, 3, 4)

    def one_block(qi):
        s = jnp.einsum('bhqd,bhkd->bhqk', qi, k).astype(jnp.float32) * scale
        p = jax.nn.softmax(s, axis=-1).astype(v.dtype)
        return jnp.einsum('bhqk,bhkd->bhqd', p, v)

    o = lax.map(one_block, qb)
    return o.transpose(1, 2, 0, 3, 4).reshape(B, H, T, v.shape[-1])


def mla_mixer(u, pos, qa_norm_g, w_uq, kva_norm_g, w_ukv, qn_g, kn_g):
    B, T, _ = u.shape
    c_q = u[..., :Q_LORA]
    c_kv = u[..., Q_LORA:Q_LORA + KV_LORA]
    k_pe = u[..., Q_LORA + KV_LORA:]
    q = split_heads(rms_norm(c_q, qa_norm_g) @ w_uq, MLA_HEADS)
    kv = split_heads(rms_norm(c_kv, kva_norm_g) @ w_ukv, MLA_HEADS)
    k_nope, v = kv[..., :MLA_NOPE], kv[..., MLA_NOPE:]
    k_pe = jnp.broadcast_to(k_pe[:, None], (B, MLA_HEADS, T, MLA_ROPE))
    k = jnp.concatenate([k_nope, k_pe], axis=-1)
    q = rms_norm(q, qn_g)
    k = rms_norm(k, kn_g)
    q = jnp.concatenate([q[..., :MLA_NOPE], apply_rope(q[..., MLA_NOPE:], pos)], axis=-1)
    k = jnp.concatenate([k[..., :MLA_NOPE], apply_rope(k[..., MLA_NOPE:], pos)], axis=-1)
    return merge_heads(block_attention(q, k, v))


def gla_mixer(u, gk_w2, gk_b, onorm_g):
    f32 = jnp.float32
    kw = GLA_HEADS * GLA_DK
    vw = GLA_HEADS * GLA_DV
    q, k, v, g, r_fwd, r_bwd = jnp.split(u, [kw, 2 * kw, 2 * kw + vw, 2 * kw + 2 * vw, 2 * kw + 2 * vw + GLA_GATE_RANK], axis=-1)
    q = split_heads(q, GLA_HEADS).astype(f32) * (GLA_DK ** -0.5)
    k = split_heads(k, GLA_HEADS).astype(f32)
    v = split_heads(v, GLA_HEADS).astype(f32)

    def log_gate(r, d):
        z = (r @ gk_w2[d] + gk_b[d]).astype(f32)
        return split_heads(jax.nn.log_sigmoid(z) / GLA_GATE_NORM, GLA_HEADS)

    o = bidirectional_scan(q, k, k, v, log_gate(r_fwd, 0), log_gate(r_bwd, 1))
    o = rms_norm(o, onorm_g) * jax.nn.silu(split_heads(g, GLA_HEADS).astype(f32))
    return merge_heads(o).astype(u.dtype)


def trunk(x, c, w):
    B, T, _ = x.shape
    pos = jnp.arange(T, dtype=jnp.int32)
    p = jax.nn.softmax(w['hgrn_lb'].astype(jnp.float32), axis=1)
    lb = jnp.cumsum(p, axis=1)
    lb = lb - lb[:, :1]
    cond = jax.nn.silu(c)
    for l in range(DEPTH):
        mod = (cond @ w['ada_w'][l] + w['ada_b'][l]).reshape(B, 1, N_SUB, 3, D_MODEL)
        shift, scale, gate = mod[:, :, :, 0], mod[:, :, :, 1], mod[:, :, :, 2]

        def adaln(x, s):
            return rms_norm(x, w['norm_g'][l, s]) * (1.0 + scale[:, :, s]) + shift[:, :, s]

        h = adaln(x, 0)
        x = x + RES_HALF * gate[:, :, 0] * swiglu(h, w['ffn_w13'][l, 0], w['ffn_w2'][l, 0])
        h = adaln(x, 1)
        e = l // 2
        if l % 2 == 0:
            u = h @ w['ev_w_in'][e]
            u_hg, u_mla = u[..., :5 * HG_WIDTH], u[..., 5 * HG_WIDTH:]
            o_hg = hgrn2_mixer(u_hg, lb[0, e], lb[1, e], w['hgrn_onorm_g'][e])
            o_mla = mla_mixer(u_mla, pos, w['mla_qa_norm_g'][e], w['mla_w_uq'][e], w['mla_kva_norm_g'][e],
                              w['mla_w_ukv'][e], w['mla_qn_g'][e], w['mla_kn_g'][e])
            y = jnp.concatenate([o_hg, o_mla], axis=-1) @ w['ev_w_out'][e]
        else:
            u = h @ w['od_w_in'][e]
            y = gla_mixer(u, w['gla_gk_w2'][e], w['gla_gk_b'][e], w['gla_onorm_g'][e]) @ w['od_w_out'][e]
        x = x + gate[:, :, 1] * y
        h = adaln(x, 2)
        x = x + RES_HALF * gate[:, :, 2] * swiglu(h, w['ffn_w13'][l, 1], w['ffn_w2'][l, 1])
    return x


def setup_inputs(seed: int = 0) -> dict:
    key = jax.random.key(seed)
    ks = jax.random.split(key, 24)
    f32 = jnp.float32
    nrm = lambda k, shape, s: jax.random.normal(k, shape, f32) * s
    gain = lambda k, shape: 1.0 + 0.05 * jax.random.normal(k, shape, f32)
    return {
        'x_prompt': nrm(ks[0], (BATCH, SEQ, D_MODEL), 1.0),
        'x_sample': nrm(ks[1], (DEC_BATCH, DEC_SEQ, D_MODEL), 1.0),
        'c_prompt': nrm(ks[2], (BATCH, D_MODEL), 1.0),
        'c_sample': nrm(ks[3], (DEC_BATCH, D_MODEL), 1.0),
        'ada_w': nrm(ks[4], (DEPTH, D_MODEL, 3 * N_SUB * D_MODEL), 0.5 * D_MODEL ** -0.5),
        'ada_b': nrm(ks[5], (DEPTH, 3 * N_SUB * D_MODEL), 0.02),
        'norm_g': gain(ks[6], (DEPTH, N_SUB, D_MODEL)),
        'ffn_w13': nrm(ks[7], (DEPTH, 2, D_MODEL, 2 * FFN_DIM), D_MODEL ** -0.5),
        'ffn_w2': nrm(ks[8], (DEPTH, 2, FFN_DIM, D_MODEL), FFN_DIM ** -0.5),
        'ev_w_in': nrm(ks[9], (N_EVEN, D_MODEL, EV_IN), D_MODEL ** -0.5),
        'ev_w_out': nrm(ks[10], (N_EVEN, EV_MIX, D_MODEL), EV_MIX ** -0.5),
        'hgrn_lb': nrm(ks[11], (2, N_EVEN, HG_WIDTH), 0.5),
        'hgrn_onorm_g': gain(ks[12], (N_EVEN, HG_DV)),
        'mla_qa_norm_g': gain(ks[13], (N_EVEN, Q_LORA)),
        'mla_w_uq': nrm(ks[14], (N_EVEN, Q_LORA, MLA_HEADS * MLA_QK), Q_LORA ** -0.5),
        'mla_kva_norm_g': gain(ks[15], (N_EVEN, KV_LORA)),
        'mla_w_ukv': nrm(ks[16], (N_EVEN, KV_LORA, MLA_HEADS * (MLA_NOPE + MLA_V)), KV_LORA ** -0.5),
        'mla_qn_g': gain(ks[17], (N_EVEN, MLA_QK)),
        'mla_kn_g': gain(ks[18], (N_EVEN, MLA_QK)),
        'od_w_in': nrm(ks[19], (N_ODD, D_MODEL, OD_IN), D_MODEL ** -0.5),
        'od_w_out': nrm(ks[20], (N_ODD, OD_MIX, D_MODEL), OD_MIX ** -0.5),
        'gla_gk_w2': nrm(ks[21], (N_ODD, 2, GLA_GATE_RANK, GLA_HEADS * GLA_DK), GLA_GATE_RANK ** -0.5),
        'gla_gk_b': nrm(ks[22], (N_ODD, 2, GLA_HEADS * GLA_DK), 0.1),
        'gla_onorm_g': gain(ks[23], (N_ODD, GLA_DV)),
    }


def reference(x_prompt, x_sample, c_prompt, c_sample, ada_w, ada_b, norm_g, ffn_w13, ffn_w2,
              ev_w_in, ev_w_out, hgrn_lb, hgrn_onorm_g, mla_qa_norm_g, mla_w_uq, mla_kva_norm_g,
              mla_w_ukv, mla_qn_g, mla_kn_g, od_w_in, od_w_out, gla_gk_w2, gla_gk_b, gla_onorm_g):
    w = {
        'ada_w': ada_w, 'ada_b': ada_b, 'norm_g': norm_g, 'ffn_w13': ffn_w13, 'ffn_w2': ffn_w2,
        'ev_w_in': ev_w_in, 'ev_w_out': ev_w_out, 'hgrn_lb': hgrn_lb, 'hgrn_onorm_g': hgrn_onorm_g,
        'mla_qa_norm_g': mla_qa_norm_g, 'mla_w_uq': mla_w_uq, 'mla_kva_norm_g': mla_kva_norm_g,
        'mla_w_ukv': mla_w_ukv, 'mla_qn_g': mla_qn_g, 'mla_kn_g': mla_kn_g,
        'od_w_in': od_w_in, 'od_w_out': od_w_out, 'gla_gk_w2': gla_gk_w2, 'gla_gk_b': gla_gk_b,
        'gla_onorm_g': gla_onorm_g,
    }
    y_prompt = trunk(x_prompt, c_prompt, w)
    y_sample = trunk(x_sample, c_sample, w)
    return (y_prompt, y_sample)
```

```python
import numpy as np
import concourse.bass as bass
import concourse.mybir as mybir
from concourse.bass_utils import run_bass_kernel_spmd

F32 = mybir.dt.float32
BF16 = mybir.dt.bfloat16
AF = mybir.ActivationFunctionType
ALU = mybir.AluOpType

D = 2048
FF = 5632
NFC = FF // 128
KC = D // 128
EPS = 1e-6
DEPTH = 4

ENGS = ["pe", "act", "dve", "pool", "sp"]


class Buf:
    __slots__ = ("name", "t", "w", "r", "sem", "cnt", "q", "nobar")

    def __init__(self, name, t=None):
        self.q = None
        self.nobar = False
        self.name = name
        self.t = t
        self.w = None
        self.r = []
        self.sem = None
        self.cnt = 0

    def __getitem__(self, idx):
        return self.t[idx]


class DramBuf:
    __slots__ = ("name", "t", "pending", "fenced")

    def __init__(self, name, t):
        self.name = name
        self.t = t
        self.pending = {}
        self.fenced = {}

    def __getitem__(self, idx):
        return self.t[idx]

    def fence(self):
        for k, v in self.pending.items():
            if self.fenced.get(k, (None, 0))[1] < v[1]:
                self.fenced[k] = v
        self.pending = {}


class Rec:
    __slots__ = ("eng", "fn", "deps", "needed", "tick", "dma")

    def __init__(self, eng, fn, deps, dma=None):
        self.eng = eng
        self.fn = fn
        self.deps = deps
        self.needed = False
        self.tick = 0
        self.dma = dma


class Prog:
    def __init__(self, nc):
        self.nc = nc
        self.ops = {e: [] for e in ENGS}
        self.last = {e: None for e in ENGS}
        self.bufs = []
        self.nsem = 0
        self.sb_off = 20480
        self.sb_base = 20480
        self.nalloc = 0
        self.sempool = {"sp": [], "pool": []}
        self.nbase = 0
        self.engsem = {e: nc.alloc_semaphore(name="prog_" + e) for e in ENGS if e != "sp"}

    def sb(self, name, shape, dtype, dma=False):
        esz = 4 if dtype == F32 else 2
        n = 1
        for s_ in shape[1:]:
            n *= s_
        nbytes = (n * esz + 63) // 64 * 64
        self.nalloc += 1
        t = self.nc.alloc_sbuf_tensor_at(
            "%s_%d" % (name, self.nalloc), list(shape), dtype, offset=self.sb_off
        )
        self.sb_off += nbytes
        assert self.sb_off <= 229376, ("SBUF overflow", name, self.sb_off)
        b = Buf(name, t)
        self.bufs.append(b)
        return b

    def _bufsem(self, b, q):
        if b.sem is None:
            b.q = q
            if self.sempool[q]:
                b.sem, b.cnt = self.sempool[q].pop()
            else:
                b.sem = self.nc.alloc_semaphore(name="bs%d_%s" % (self.nsem, b.name))
                self.nsem += 1
        assert b.q == q, ("buffer DMA'd from two queue kinds", b.name)
        return b.sem

    def phase_mark(self):
        self.sb_base = self.sb_off
        self.nbase = len(self.bufs)

    def phase_reset(self):
        self.barrier()
        self.sb_off = self.sb_base
        for b in self.bufs[self.nbase:]:
            if b.sem is not None:
                self.sempool[b.q].append((b.sem, b.cnt))
        del self.bufs[self.nbase:]

    def _collect(self, eng, reads, writes):
        deps = []
        for b in reads:
            if b.w is not None:
                deps.append(b.w)
        for b in writes:
            if b.w is not None:
                deps.append(b.w)
            deps.extend(b.r)
        out = []
        for d in deps:
            if isinstance(d, Rec):
                if d.eng == eng and eng == "pe":
                    continue
                d.needed = True
            out.append(d)
        return out

    def op(self, eng, fn, reads=(), writes=()):
        deps = self._collect(eng, reads, writes)
        r = Rec(eng, fn, deps)
        self.ops[eng].append(r)
        self.last[eng] = r
        for b in reads:
            b.r.append(r)
        for b in writes:
            b.w = r
            b.r = []
        return r

    def dma(self, q, out, in_, sembuf, reads=(), writes=(), dram_r=(), dram_w=()):
        deps = self._collect(q, reads, writes)
        for db in list(dram_r) + list(dram_w):
            deps.extend(db.fenced.values())
        sem = self._bufsem(sembuf, q)
        sembuf.cnt += 1
        ev = (sem, 16 * sembuf.cnt)
        r = Rec(q, lambda e: e.dma_start(out=out, in_=in_), deps, dma=ev)
        self.ops[q].append(r)
        for b in reads:
            b.r.append(ev)
        for b in writes:
            b.w = ev
            b.r = []
        for db in dram_w:
            db.pending[id(sem)] = ev
        return r

    def barrier(self):
        evs = []
        for e in ENGS:
            if e != "sp" and self.last[e] is not None:
                self.last[e].needed = True
                evs.append(self.last[e])
        for b in self.bufs:
            if b.sem is not None and b.cnt > 0 and not b.nobar:
                evs.append((b.sem, 16 * b.cnt))
        for e in ENGS:
            deps = [d for d in evs if not (isinstance(d, Rec) and d.eng == e)]
            self.ops[e].append(Rec(e, None, deps))
        for b in self.bufs:
            b.w = None
            b.r = []

    def emit(self):
        nc = self.nc
        for e in ENGS:
            t = 0
            for r in self.ops[e]:
                if r.dma is None and r.needed:
                    t += 1
                    r.tick = t
        engsem = self.engsem
        ops = self.ops

        def run(e, h):
            seen = {}
            for r in ops[e]:
                waits = {}
                for d in r.deps:
                    if isinstance(d, Rec):
                        sem, val = engsem[d.eng], d.tick
                    else:
                        sem, val = d
                    k = id(sem)
                    if seen.get(k, 0) >= val:
                        continue
                    if k not in waits or waits[k][1] < val:
                        waits[k] = (sem, val)
                for k, (sem, val) in waits.items():
                    h.wait_ge(sem, val)
                    seen[k] = val
                if r.fn is None:
                    continue
                ins = r.fn(h)
                if r.dma is not None:
                    ins.then_inc(r.dma[0], 16)
                elif r.needed:
                    ins.then_inc(engsem[e], 1)

        with nc.Block() as block:

            @block.tensor
            def _(h):
                run("pe", h)

            @block.scalar
            def _(h):
                run("act", h)

            @block.vector
            def _(h):
                run("dve", h)

            @block.gpsimd
            def _(h):
                run("pool", h)

            @block.sync
            def _(h):
                run("sp", h)


class Builder:
    def __init__(self, T, layers, ffn_only=False, dbg=None, TT=512):
        self.T = T
        self.layers = layers
        self.TT = TT
        self.ffn_only = ffn_only
        nc = bass.Bass("TRN2", target_bir_lowering=False)
        self.nc = nc
        self.P = Prog(nc)
        P = self.P
        L = DEPTH

        def din(name, shape, dt=F32):
            return nc.dram_tensor(name, list(shape), dt, kind="ExternalInput").ap()

        self.x = din("x", [T, D])
        self.y = DramBuf("y", nc.dram_tensor("y", [T, D], F32, kind="ExternalOutput").ap())
        self.c_col = din("c_col", [128, KC])
        self.ada_w = {l: din("ada_w_%d" % l, [D, 9 * D]) for l in layers}
        self.ada_b_col = din("ada_b_col", [L, 128, 144])
        self.norm_g_col = din("norm_g_col", [L, 128, 3 * KC])
        self.ffn_w13 = {l: din("ffn_w13_%d" % l, [2, D, 2 * FF]) for l in layers}
        self.ffn_w2 = {l: din("ffn_w2_%d" % l, [2, FF, D]) for l in layers}
        self.ident_d = din("ident", [128, 128])
        NB = T // 128
        self.tmask_d = din("tmask_col", [128, NB])
        self.kbias_d = din("kbias_col", [128, NB])
        self.tmask_row_d = din("tmask_row", [1, T])
        self.masks_d = din("masks", [6, 128, 128])
        self.od_w_in = din("od_w_in", [2, D, 6176])
        self.od_w_out = din("od_w_out", [2, D, D])
        self.gla_w2 = din("gla_gk_w2", [2, 2, 16, 1024])
        self.gla_b_col = din("gla_gk_b_col", [2, 2, 128, 8])
        self.gla_ong_col = din("gla_onorm_g_col", [2, 128, 4])
        self.ev_w_in = din("ev_w_in", [2, D, 6208])
        self.ev_w_out = din("ev_w_out", [2, D, D])
        self.hg_lb_col = din("hgrn_lb_col", [2, 2, 128, 8])
        self.hg_ong_col = din("hgrn_onorm_g_col", [2, 128, 1])
        self.mla_qa_col = din("mla_qa_norm_g_col", [2, 128, 4])
        self.mla_kva_col = din("mla_kva_norm_g_col", [2, 128, 4])
        self.mla_w_uq = din("mla_w_uq", [2, 512, 1536])
        self.mla_w_ukv = din("mla_w_ukv", [2, 512, 2048])
        self.mla_qn_col = din("mla_qn_g_col", [2, 128, 2])
        self.mla_kn_col = din("mla_kn_g_col", [2, 128, 2])
        self.rope_d = din("rope_tab", [2, 64, T])
        self.pswap_d = din("pswap", [64, 64])
        self.winb = {}
        self.woutb = {}
        for l in layers:
            nin = 6208 if l % 2 == 0 else 6176
            self.winb[l] = DramBuf("winb", nc.dram_tensor("winb_%d" % l, [D, nin], BF16).ap())
            self.woutb[l] = DramBuf("woutb", nc.dram_tensor("woutb_%d" % l, [D, D], BF16).ap())
        self.wuqb = {}
        self.wukvb = {}
        for l in layers:
            if l % 2 == 0:
                self.wuqb[l] = DramBuf("wuqb", nc.dram_tensor("wuqb_%d" % l, [512, 1536], BF16).ap())
                self.wukvb[l] = DramBuf("wukvb", nc.dram_tensor("wukvb_%d" % l, [512, 2048], BF16).ap())
        self.UT = DramBuf("UT", nc.dram_tensor("UT", [6144, T], F32).ap())
        self.V = DramBuf("V", nc.dram_tensor("Vs", [T, 2048], F32).ap())
        self.OF = DramBuf("OF", nc.dram_tensor("OFs", [2048, T], F32).ap())
        self.MIXT = DramBuf("MIXT", nc.dram_tensor("MIXT", [2048, T], BF16).ap())
        self.QN = DramBuf("QN", nc.dram_tensor("QN", [8, 128, T], BF16).ap())
        self.QR = DramBuf("QR", nc.dram_tensor("QR", [8, 64, T], BF16).ap())
        self.KN = DramBuf("KN", nc.dram_tensor("KN", [8, 128, T], BF16).ap())
        self.KR = DramBuf("KR", nc.dram_tensor("KR", [8, 64, T], BF16).ap())
        self.VA = DramBuf("VA", nc.dram_tensor("VA", [T, 1024], BF16).ap())

        self.w13b = {}
        self.w2b = {}
        for l in layers:
            for s in range(2):
                self.w13b[(l, s)] = DramBuf(
                    "w13b", nc.dram_tensor("w13b_%d_%d" % (l, s), [NFC, 128, KC, 256], BF16).ap()
                )
                self.w2b[(l, s)] = DramBuf(
                    "w2b", nc.dram_tensor("w2b_%d_%d" % (l, s), [FF, D], BF16).ap()
                )

        self.ident = P.sb("ident", [128, 128], F32)
        self.ones = P.sb("ones", [128, 128], F32)
        self.modc = P.sb("modc", [128, L * 144], F32)
        self.gsc = P.sb("gsc", [128, L * 3 * KC], F32)
        self.ccol = P.sb("ccol", [128, KC], F32)
        self.epsc = P.sb("epsc", [128, 1], F32)
        self.onec = P.sb("onec", [128, 1], F32)
        self.tmask = P.sb("tmask", [128, NB], F32)
        self.kbias = P.sb("kbias", [128, NB], F32)
        self.masks = P.sb("masks", [128, 6, 128], F32)
        self.ngc = P.sb("ngc", [128, L * 3 * KC], F32)
        self.abc = P.sb("abc", [128, L * 144], F32)
        self.wcvs = {}
        for l in layers:
            wb_ = Buf("wconv%d" % l)
            wb_.nobar = True
            P.bufs.append(wb_)
            self.wcvs[l] = wb_
        self.ps = []
        for i in range(8):
            t = nc.alloc_psum_tensor("psb%d" % i, [128, 512], F32)
            b = Buf("ps%d" % i, t)
            P.bufs.append(b)
            self.ps.append(b)
        self.psi = 0
        self.psa = 0
        self.psb = 0
        P.phase_mark()

        self.build()
        P.barrier()
        P.emit()

    def nps(self):
        b = self.ps[self.psi % 8]
        self.psi += 1
        return b

    def nps_a(self):
        b = self.ps[self.psa % 4]
        self.psa += 1
        return b

    def nps_b(self):
        b = self.ps[4 + self.psb % 4]
        self.psb += 1
        return b

    def build(self):
        self.phase_consts()
        self.phase_wconv()
        self.phase_mod()
        first = True
        for l in self.layers:
            self.phase_ffn(l, 0, first)
            first = False
            if not self.ffn_only:
                self.phase_mixer(l)
            self.phase_ffn(l, 1, False)

    def phase_consts(self):
        P = self.P
        P.dma("sp", self.ident[:], self.ident_d[:, :], self.ident, writes=[self.ident])
        P.dma("sp", self.ccol[:], self.c_col[:, :], self.ccol, writes=[self.ccol])
        P.dma("sp", self.ngc[:].rearrange("p (l k) -> p l k", l=DEPTH),
              self.norm_g_col.rearrange("l p k -> p l k"), self.ngc, writes=[self.ngc])
        P.dma("sp", self.abc[:].rearrange("p (l k) -> p l k", l=DEPTH),
              self.ada_b_col.rearrange("l p k -> p l k"), self.abc, writes=[self.abc])
        P.op("dve", lambda e: e.memset(self.ones[:], 1.0), writes=[self.ones])
        P.op("dve", lambda e: e.memset(self.epsc[:], EPS), writes=[self.epsc])
        P.op("dve", lambda e: e.memset(self.onec[:], 1.0), writes=[self.onec])
        P.dma("sp", self.tmask[:], self.tmask_d[:, :], self.tmask, writes=[self.tmask])
        P.dma("sp", self.kbias[:], self.kbias_d[:, :], self.kbias, writes=[self.kbias])
        P.dma("sp", self.masks[:], self.masks_d.rearrange("m p c -> p m c"), self.masks, writes=[self.masks])

    def phase_wconv(self):
        P = self.P
        for l in self.layers:
            self.wcv = self.wcvs[l]
            for s in range(2):
                w13 = self.ffn_w13[l][s]
                dst = self.w13b[(l, s)]
                for half in range(2):
                    for j in range(NFC):
                        src = w13[:, half * FF + j * 128: half * FF + (j + 1) * 128].rearrange(
                            "(kc p) c -> p kc c", p=128)
                        P.dma("pool", dst.t[j, :, :, half * 128:(half + 1) * 128], src,
                              self.wcv, dram_w=[dst])
                w2 = self.ffn_w2[l][s]
                dst2 = self.w2b[(l, s)]
                for r0 in range(0, FF, 512):
                    P.dma("pool", dst2.t[r0:r0 + 512, :], w2[r0:r0 + 512, :], self.wcv, dram_w=[dst2])
            if not self.ffn_only:
                src_in = self.ev_w_in[l // 2] if l % 2 == 0 else self.od_w_in[l // 2]
                src_out = self.ev_w_out[l // 2] if l % 2 == 0 else self.od_w_out[l // 2]
                for r0 in range(0, D, 512):
                    P.dma("pool", self.winb[l].t[r0:r0 + 512, :], src_in[r0:r0 + 512, :], self.wcv,
                          dram_w=[self.winb[l]])
                    P.dma("pool", self.woutb[l].t[r0:r0 + 512, :], src_out[r0:r0 + 512, :], self.wcv,
                          dram_w=[self.woutb[l]])
                if l % 2 == 0:
                    P.dma("pool", self.wuqb[l].t[:, :], self.mla_w_uq[l // 2], self.wcv, dram_w=[self.wuqb[l]])
                    P.dma("pool", self.wukvb[l].t[:, :], self.mla_w_ukv[l // 2], self.wcv, dram_w=[self.wukvb[l]])
            dbs = [self.w13b[(l, 0)], self.w13b[(l, 1)], self.w2b[(l, 0)], self.w2b[(l, 1)]]
            if not self.ffn_only:
                dbs += [self.winb[l], self.woutb[l]]
                if l % 2 == 0:
                    dbs += [self.wuqb[l], self.wukvb[l]]
            ev = (self.wcv.sem, 16 * self.wcv.cnt)
            for db in dbs:
                db.pending = {id(self.wcv.sem): ev}
                db.fence()

    def phase_mod(self):
        P = self.P
        CB = 512
        cond = P.sb("cond", [128, KC], F32)
        P.op("act", lambda e: e.activation(out=cond[:], in_=self.ccol[:], func=AF.Silu),
             reads=[self.ccol], writes=[cond])
        slots = [P.sb("adaw%d" % i, [128, KC, CB], F32) for i in range(2)]
        n = 0
        for l in self.layers:
            pst = self.nps()
            for cb in range(9 * D // CB):
                sl = slots[n % 2]
                n += 1
                src = self.ada_w[l][:, cb * CB:(cb + 1) * CB].rearrange("(kc p) c -> p kc c", p=128)
                P.dma("sp", sl[:], src, sl, writes=[sl])
                for jj in range(CB // 128):
                    j = cb * (CB // 128) + jj
                    for kc in range(KC):
                        P.op("pe", lambda e, sl=sl, kc=kc, jj=jj, j=j, pst=pst: e.matmul(
                            pst[:, j:j + 1], lhsT=sl[:, kc, jj * 128:(jj + 1) * 128],
                            rhs=cond[:, kc:kc + 1], start=(kc == 0), stop=(kc == KC - 1)),
                            reads=[sl, cond], writes=[pst])
            P.op("dve", lambda e, l=l, pst=pst: e.tensor_tensor(
                out=self.modc[:, l * 144:(l + 1) * 144], in0=pst[:, 0:144],
                in1=self.abc[:, l * 144:(l + 1) * 144], op=ALU.add),
                reads=[pst, self.abc], writes=[self.modc])
            for s in range(3):
                sc = self.modc[:, l * 144 + (s * 3 + 1) * KC: l * 144 + (s * 3 + 2) * KC]
                o = (l * 3 + s) * KC
                P.op("dve", lambda e, sc=sc, o=o: e.scalar_tensor_tensor(
                    out=self.gsc[:, o:o + KC], in0=sc, scalar=1.0, in1=self.ngc[:, o:o + KC],
                    op0=ALU.add, op1=ALU.mult),
                    reads=[self.modc, self.ngc], writes=[self.gsc])
        P.phase_reset()

    def mod_col(self, l, s, j):
        o = l * 144 + (s * 3 + j) * KC
        return self.modc[:, o:o + KC]

    def make_gate_b(self, l, s, mult):
        P = self.P
        gb = P.sb("gate_b", [128, D], F32)
        gtmp = [P.sb("gtmp%d" % i, [128, 128], F32) for i in range(2)]
        gc = self.mod_col(l, s, 2)
        for q in range(4):
            pst = self.nps()
            for kk in range(4):
                kc = q * 4 + kk
                g = gtmp[kc % 2]
                P.op("dve", lambda e, g=g, kc=kc: e.tensor_scalar(
                    out=g[:], in0=self.ident[:], scalar1=gc[:, kc:kc + 1], scalar2=None, op0=ALU.mult),
                    reads=[self.ident, self.modc], writes=[g])
                P.op("pe", lambda e, g=g, kk=kk, pst=pst: e.matmul(
                    pst[:, kk * 128:(kk + 1) * 128], lhsT=self.ones[:], rhs=g[:], start=True, stop=True),
                    reads=[self.ones, g], writes=[pst])
            P.op("act", lambda e, q=q, pst=pst: e.activation(
                out=gb[:, q * 512:(q + 1) * 512], in_=pst[:], func=AF.Copy, scale=float(mult)),
                reads=[pst], writes=[gb])
        return gb

    def norm_transpose(self, xt, hT, col0, gs, sh, xn_slots, cnt, stats):
        P = self.P
        ss, rstd = stats[cnt % len(stats)]
        xn = xn_slots[cnt % len(xn_slots)]
        P.op("act", lambda e: e.activation(out=xn[:], in_=xt[:], func=AF.Square, accum_out=ss[:]),
             reads=[xt], writes=[xn, ss])
        P.op("act", lambda e: e.activation(out=ss[:], in_=ss[:], func=AF.Sqrt, bias=self.epsc[:, 0:1],
                                            scale=1.0 / D), reads=[ss, self.epsc], writes=[ss])
        P.op("dve", lambda e: e.reciprocal(out=rstd[:], in_=ss[:]), reads=[ss], writes=[rstd])
        P.op("pool", lambda e: e.tensor_scalar(out=xn[:], in0=xt[:], scalar1=rstd[:, 0:1], scalar2=None,
                                                op0=ALU.mult), reads=[xt, rstd], writes=[xn])
        for q in range(4):
            pst = self.nps()
            for kk in range(4):
                kc = q * 4 + kk
                P.op("pe", lambda e, kc=kc, kk=kk, pst=pst: e.transpose(
                    pst[:, kk * 128:(kk + 1) * 128], xn[:, kc * 128:(kc + 1) * 128], self.ident[:]),
                    reads=[xn, self.ident], writes=[pst])
            for kk in range(4):
                kc = q * 4 + kk
                if q % 2 == 0:
                    P.op("act", lambda e, kc=kc, kk=kk, pst=pst: e.activation(
                        out=hT[:, kc, col0:col0 + 128], in_=pst[:, kk * 128:(kk + 1) * 128],
                        func=AF.Identity, bias=sh[:, kc:kc + 1], scale=gs[:, kc:kc + 1]),
                        reads=[pst, self.modc, self.gsc], writes=[hT])
                else:
                    P.op("dve", lambda e, kc=kc, kk=kk, pst=pst: e.tensor_scalar(
                        out=hT[:, kc, col0:col0 + 128], in0=pst[:, kk * 128:(kk + 1) * 128],
                        scalar1=gs[:, kc:kc + 1], scalar2=sh[:, kc:kc + 1], op0=ALU.mult, op1=ALU.add),
                        reads=[pst, self.modc, self.gsc], writes=[hT])

    def phase_ffn(self, l, s, first):
        P = self.P
        T, TT = self.T, self.TT
        TS = TT // 128
        sub = 0 if s == 0 else 2
        w13b = self.w13b[(l, s)]
        w2b = self.w2b[(l, s)]
        xin = self.x if first else self.y.t
        self.y.fence()
        gs = self.gsc[:, (l * 3 + sub) * KC:(l * 3 + sub + 1) * KC]
        sh = self.mod_col(l, sub, 0)
        gb = self.make_gate_b(l, sub, 0.5)
        hT = P.sb("hT", [128, KC, TT], BF16)
        actT = P.sb("actT", [128, NFC, TT], BF16)
        xts = [P.sb("xt%d" % i, [128, D], F32) for i in range(TS)]
        xns = [P.sb("xn%d" % i, [128, D], F32) for i in range(2)]
        w13s = [P.sb("w13s%d" % i, [128, KC, 256], BF16) for i in range(5)]
        w2s = [P.sb("w2s%d" % i, [128, 4, 512], BF16) for i in range(6)]
        sgs = [P.sb("sg%d" % i, [128, 512], F32) for i in range(2)]
        yts = [P.sb("yt%d" % i, [128, 1024], F32) for i in range(2)]
        stats = [(P.sb("ss%d" % i, [128, 1], F32), P.sb("rstd%d" % i, [128, 1], F32)) for i in range(2)]
        n13 = 0
        n2 = 0
        nsg = 0
        nyt = 0
        cnt = 0
        for t0 in range(0, T, TT):
            for si in range(TS):
                xt = xts[si]
                P.dma("sp", xt[:], xin[t0 + si * 128:t0 + (si + 1) * 128, :], xt, writes=[xt],
                      dram_r=[self.y])
                self.norm_transpose(xt, hT, si * 128, gs, sh, xns, cnt, stats)
                cnt += 1
            for j in range(NFC):
                wsl = w13s[n13 % 5]
                n13 += 1
                P.dma("sp", wsl[:], w13b.t[j], wsl, writes=[wsl], dram_r=[w13b])
                for c0 in range(0, TT, 512):
                    pg = self.nps()
                    pu = self.nps()
                    for kc in range(KC):
                        P.op("pe", lambda e, wsl=wsl, kc=kc, pg=pg, c0=c0: e.matmul(
                            pg[:], lhsT=wsl[:, kc, 0:128], rhs=hT[:, kc, c0:c0 + 512],
                            start=(kc == 0), stop=(kc == KC - 1)), reads=[wsl, hT], writes=[pg])
                    for kc in range(KC):
                        P.op("pe", lambda e, wsl=wsl, kc=kc, pu=pu, c0=c0: e.matmul(
                            pu[:], lhsT=wsl[:, kc, 128:256], rhs=hT[:, kc, c0:c0 + 512],
                            start=(kc == 0), stop=(kc == KC - 1)), reads=[wsl, hT], writes=[pu])
                    sg = sgs[nsg % 2]
                    nsg += 1
                    P.op("act", lambda e, sg=sg, pg=pg: e.activation(out=sg[:], in_=pg[:], func=AF.Silu),
                         reads=[pg], writes=[sg])
                    P.op("dve", lambda e, sg=sg, pu=pu, j=j, c0=c0: e.tensor_tensor(
                        out=actT[:, j, c0:c0 + 512], in0=sg[:], in1=pu[:], op=ALU.mult),
                        reads=[sg, pu], writes=[actT])
            for dh in range(2):
                banks = [[self.nps(), self.nps()] for _ in range(TS)] if TS <= 4 else None
                assert banks is not None
                for jg in range(0, NFC, 4):
                    wsl = w2s[n2 % 6]
                    wsl2 = w2s[(n2 + 1) % 6]
                    n2 += 2
                    for dq, w in ((0, wsl), (1, wsl2)):
                        c0 = dh * 1024 + dq * 512
                        P.dma("sp", w[:], w2b.t[jg * 128:(jg + 4) * 128, c0:c0 + 512].rearrange(
                            "(j p) c -> p j c", p=128), w, writes=[w], dram_r=[w2b])
                    for jj in range(4):
                        j = jg + jj
                        for si in range(TS):
                            for dq, w in ((0, wsl), (1, wsl2)):
                                pb = banks[si][dq]
                                P.op("pe", lambda e, w=w, jj=jj, j=j, si=si, pb=pb: e.matmul(
                                    pb[:], lhsT=actT[:, j, si * 128:(si + 1) * 128], rhs=w[:, jj, :],
                                    start=(j == 0), stop=(j == NFC - 1)), reads=[w, actT], writes=[pb])
                for si in range(TS):
                    yt = yts[nyt % 2]
                    nyt += 1
                    for dq in range(2):
                        c0 = dh * 1024 + dq * 512
                        pb = banks[si][dq]
                        P.op("dve", lambda e, pb=pb, yt=yt, dq=dq, c0=c0, si=si: e.tensor_tensor(
                            out=yt[:, dq * 512:(dq + 1) * 512], in0=pb[:], in1=gb[:, c0:c0 + 512],
                            op=ALU.mult), reads=[pb, gb], writes=[yt])
                        P.op("pool", lambda e, yt=yt, dq=dq, c0=c0, si=si: e.tensor_tensor(
                            out=yt[:, dq * 512:(dq + 1) * 512], in0=yt[:, dq * 512:(dq + 1) * 512],
                            in1=xts[si][:, c0:c0 + 512], op=ALU.add), reads=[yt, xts[si]], writes=[yt])
                    P.dma("pool", self.y.t[t0 + si * 128:t0 + (si + 1) * 128, dh * 1024:(dh + 1) * 1024],
                          yt[:], yt, reads=[yt], dram_w=[self.y])
        P.phase_reset()

    def phase_mixer(self, l):
        import os
        skip = os.environ.get("K_SKIP", "")
        if l % 2 == 0 and skip:
            if "inproj" not in skip:
                self.phase_inproj_ev(l)
            if "scan" not in skip:
                self.phase_scan(l, dict(H=8, ndk=1, ndv=1, C=32, q=0, kf=1024, kb=2048, sg=5120, lff=3072,
                                        lfb=4096, ong=self.hg_ong_col[l // 2], mix0=0, hgroup=4))
            if "attn" not in skip:
                self.phase_attn(l)
            if "outproj" not in skip:
                self.phase_outproj(l)
            return
        if l % 2 == 1:
            self.phase_inproj_gla(l)
            self.phase_scan(l, dict(H=4, ndk=2, ndv=4, C=128, q=0, kf=1024, kb=1024, sg=2048, lff=4096,
                                    lfb=5120, ong=self.gla_ong_col[l // 2], mix0=0, hgroup=2))
        else:
            self.phase_inproj_ev(l)
            self.phase_scan(l, dict(H=8, ndk=1, ndv=1, C=32, q=0, kf=1024, kb=2048, sg=5120, lff=3072,
                                    lfb=4096, ong=self.hg_ong_col[l // 2], mix0=0, hgroup=4))
            self.phase_attn(l)
        self.phase_outproj(l)

    def load_norm_tile(self, l, t0, hT, xts, xns, stats, cnt0):
        P = self.P
        gs = self.gsc[:, (l * 3 + 1) * KC:(l * 3 + 2) * KC]
        sh = self.mod_col(l, 1, 0)
        for si in range(4):
            xt = xts[si % len(xts)]
            P.dma("sp", xt[:], self.y.t[t0 + si * 128:t0 + (si + 1) * 128, :], xt, writes=[xt],
                  dram_r=[self.y])
            self.norm_transpose(xt, hT, si * 128, gs, sh, xns, cnt0 + si, stats)

    def fm_mm(self, pst, wblk, c0, nc_, hT, prow=None):
        P = self.P
        for kc in range(KC):
            P.op("pe", lambda e, kc=kc: e.matmul(
                pst[0:nc_, :], lhsT=wblk[:, kc, c0:c0 + nc_], rhs=hT[:, kc, :],
                start=(kc == 0), stop=(kc == KC - 1)), reads=[wblk, hT], writes=[pst])

    def tm_mm(self, pst, wblk, si, hT, ncols=512):
        P = self.P
        for kc in range(KC):
            P.op("pe", lambda e, kc=kc: e.matmul(
                pst[:, 0:ncols], lhsT=hT[:, kc, si * 128:(si + 1) * 128], rhs=wblk[:, kc, 0:ncols],
                start=(kc == 0), stop=(kc == KC - 1)), reads=[wblk, hT], writes=[pst])

    def store_fm(self, st, row0, t0, nrows=128):
        self.P.dma("pool", self.UT.t[row0:row0 + nrows, t0:t0 + 512], st[0:nrows, :], st, reads=[st],
                   dram_w=[self.UT])

    def phase_inproj_gla(self, l):
        P = self.P
        T = self.T
        e_ = l // 2
        winb = self.winb[l]
        self.y.fence()
        self.UT.fence(); self.V.fence()
        hT = P.sb("hT", [128, KC, 512], BF16)
        xts = [P.sb("xt%d" % i, [128, D], F32) for i in range(2)]
        xns = [P.sb("xn%d" % i, [128, D], F32) for i in range(2)]
        stats = [(P.sb("ss%d" % i, [128, 1], F32), P.sb("rstd%d" % i, [128, 1], F32)) for i in range(2)]
        wbl = [P.sb("wbl%d" % i, [128, KC, 512], BF16) for i in range(3)]
        sts = [P.sb("st%d" % i, [128, 512], F32) for i in range(4)]
        rT = [P.sb("rT%d" % i, [16, 512], F32) for i in range(2)]
        w2s = P.sb("gkw2", [16, 2, 1024], F32)
        bcol = P.sb("gkb", [128, 16], F32)
        nbcol = P.sb("ngkb", [128, 16], F32)
        e1 = [P.sb("e1_%d" % i, [128, 512], F32) for i in range(2)]
        mrow = P.sb("mrow", [128, 512], F32)
        P.dma("sp", w2s[:], self.gla_w2[e_].rearrange("d r c -> r d c"), w2s, writes=[w2s])
        P.dma("sp", bcol[:].rearrange("p (d k) -> p d k", d=2), self.gla_b_col[e_].rearrange("d p k -> p d k"),
              bcol, writes=[bcol])
        P.op("dve", lambda e: e.tensor_scalar(out=nbcol[:], in0=bcol[:], scalar1=-1.0, scalar2=None,
                                               op0=ALU.mult), reads=[bcol], writes=[nbcol])
        nw = 0
        nst = 0
        cnt = 0
        for t0 in range(0, T, 512):
            self.load_norm_tile(l, t0, hT, xts, xns, stats, cnt)
            cnt += 4
            P.dma("sp", mrow[:], self.tmask_row_d[0:1, t0:t0 + 512].partition_broadcast(128), mrow, writes=[mrow])
            for b in range(13):
                wb = wbl[nw % 3]
                nw += 1
                ncol = 512 if b < 12 else 32
                P.dma("sp", wb[:, :, 0:ncol], winb.t[:, b * 512:b * 512 + ncol].rearrange(
                    "(kc p) c -> p kc c", p=128), wb, writes=[wb], dram_r=[winb])
                if b < 4 or (8 <= b < 12):
                    for cc in range(4):
                        pst = self.nps()
                        self.fm_mm(pst, wb, cc * 128, 128, hT)
                        st = sts[nst % 4]
                        nst += 1
                        if b < 2:
                            P.op("act", lambda e, st=st, pst=pst: e.activation(
                                out=st[:], in_=pst[:], func=AF.Copy, scale=1.0 / 16.0), reads=[pst], writes=[st])
                            row0 = b * 512 + cc * 128
                        elif b < 4:
                            P.op("dve", lambda e, st=st, pst=pst: e.tensor_copy(out=st[:], in_=pst[:]),
                                 reads=[pst], writes=[st])
                            row0 = b * 512 + cc * 128
                        else:
                            P.op("act", lambda e, st=st, pst=pst: e.activation(
                                out=st[:], in_=pst[:], func=AF.Silu), reads=[pst], writes=[st])
                            row0 = 2048 + (b - 8) * 512 + cc * 128
                        self.store_fm(st, row0, t0)
                elif b < 8:
                    for si in range(4):
                        pst = self.nps()
                        self.tm_mm(pst, wb, si, hT)
                        st = sts[nst % 4]
                        nst += 1
                        ti = (t0 // 128) + si
                        P.op("act", lambda e, st=st, pst=pst, ti=ti: e.activation(
                            out=st[:], in_=pst[:], func=AF.Copy, scale=self.tmask[:, ti:ti + 1]),
                            reads=[pst, self.tmask], writes=[st])
                        P.dma("pool", self.V.t[t0 + si * 128:t0 + (si + 1) * 128, (b - 4) * 512:(b - 3) * 512],
                              st[:], st, reads=[st], dram_w=[self.V])
                else:
                    for d_ in range(2):
                        pst = self.nps()
                        self.fm_mm(pst, wb, d_ * 16, 16, hT)
                        P.op("dve", lambda e, d_=d_, pst=pst: e.tensor_copy(out=rT[d_][:], in_=pst[0:16, :]),
                             reads=[pst], writes=[rT[d_]])
                    for d_ in range(2):
                        for cc in range(8):
                            pst = self.nps()
                            P.op("pe", lambda e, d_=d_, cc=cc, pst=pst: e.matmul(
                                pst[:], lhsT=w2s[:, d_, cc * 128:(cc + 1) * 128], rhs=rT[d_][:],
                                start=True, stop=True), reads=[w2s, rT[d_]], writes=[pst])
                            ee = e1[cc % 2]
                            P.op("act", lambda e, d_=d_, cc=cc, pst=pst, ee=ee: e.activation(
                                out=ee[:], in_=pst[:], func=AF.Exp, scale=-1.0,
                                bias=nbcol[:, d_ * 8 + cc:d_ * 8 + cc + 1]), reads=[pst, nbcol], writes=[ee])
                            P.op("act", lambda e, ee=ee: e.activation(
                                out=ee[:], in_=ee[:], func=AF.Ln, bias=self.onec[:, 0:1]),
                                reads=[ee, self.onec], writes=[ee])
                            st = sts[nst % 4]
                            nst += 1
                            P.op("dve", lambda e, st=st, ee=ee: e.scalar_tensor_tensor(
                                out=st[:], in0=ee[:], scalar=-1.0 / 16.0, in1=mrow[:], op0=ALU.mult, op1=ALU.mult),
                                reads=[ee, mrow], writes=[st])
                            self.store_fm(st, 4096 + d_ * 1024 + cc * 128, t0)
        P.phase_reset()

    def phase_scan(self, l, cfg):
        P = self.P
        T = self.T
        NBLK = T // 128
        H, ndk, ndv, C = cfg["H"], cfg["ndk"], cfg["ndv"], cfg["C"]
        nsub = 128 // C
        dv = ndv * 128
        mi = 0 if C == 128 else 3
        maskf = self.masks[:, mi + 0, :]
        maskb = self.masks[:, mi + 1, :]
        rm = self.masks[:, mi + 2, :]
        self.UT.fence(); self.V.fence(); self.OF.fence(); self.MIXT.fence()
        ong = P.sb("ong", [128, ndv], F32)
        P.dma("sp", ong[:], cfg["ong"], ong, writes=[ong])
        G = cfg["hgroup"]
        NS = 2

        def mk(name, shape, dt=F32):
            return [[P.sb("%s_%d_%d" % (name, g, i), shape, dt) for i in range(NS)] for g in range(G)]

        qs, ks, lfs = mk("q", [128, ndk, 128]), mk("k", [128, ndk, 128]), mk("lf", [128, ndk, 128])
        cums, args = mk("cum", [128, ndk, 128]), mk("arg", [128, ndk, 128])
        Eqs, Eks = mk("Eq", [128, ndk, 128]), mk("Ek", [128, ndk, 128])
        v128 = mk("v128", [128, dv])
        vsub = mk("vsub", [C, nsub, dv]) if nsub > 1 else None
        els = mk("el", [128, ndk, nsub])
        ktm = mk("ktm", [128, ndk * 128])
        att = mk("att", [128, 128])
        osb = mk("osb", [128, dv])
        sqb = mk("sqb", [128, dv])
        sgb = mk("sgb", [128, ndv, 128])
        ofb = mk("ofb", [128, ndv, 128])
        rsb = mk("rsb", [128, 128])
        mxb = mk("mxb", [128, ndv, 128], BF16)
        S = [[P.sb("S_%d_%d" % (g, kk), [128, dv], F32) for kk in range(ndk)] for g in range(G)]
        S1 = [[P.sb("S1_%d_%d" % (g, kk), [128, dv], F32) for kk in range(ndk)] for g in range(G)]
        cnts = [0] * G

        def block(g, h, dirn, bi):
            n = cnts[g] % NS
            cnts[g] += 1
            t0 = bi * 128
            q, k, lf, cum, arg, Eq, Ek = qs[g][n], ks[g][n], lfs[g][n], cums[g][n], args[g][n], Eqs[g][n], Eks[g][n]
            V1 = v128[g][n]
            el = els[g][n]
            krow = cfg["kf"] if dirn == 0 else cfg["kb"]
            lrow = cfg["lff"] if dirn == 0 else cfg["lfb"]

            def ld(dst, row):
                P.dma("sp", dst[:], self.UT.t[row + h * ndk * 128: row + (h + 1) * ndk * 128, t0:t0 + 128].rearrange(
                    "(k p) t -> p k t", p=128), dst, writes=[dst], dram_r=[self.UT])
            ld(q, cfg["q"]); ld(k, krow); ld(lf, lrow)
            P.dma("sp", V1[:], self.V.t[t0:t0 + 128, h * dv:(h + 1) * dv], V1, writes=[V1], dram_r=[self.V])
            if nsub > 1:
                Vs = vsub[g][n]
                P.dma("sp", Vs[:], self.V.t[t0:t0 + 128, h * dv:(h + 1) * dv].rearrange("(c p) d -> p c d", p=C),
                      Vs, writes=[Vs], dram_r=[self.V])
            if dirn == 1:
                sg_, of_ = sgb[g][n], ofb[g][n]
                P.dma("sp", sg_[:], self.UT.t[cfg["sg"] + h * dv: cfg["sg"] + (h + 1) * dv, t0:t0 + 128].rearrange(
                    "(k p) t -> p k t", p=128), sg_, writes=[sg_], dram_r=[self.UT])
                P.dma("sp", of_[:], self.OF.t[h * dv:(h + 1) * dv, t0:t0 + 128].rearrange("(k p) t -> p k t", p=128),
                      of_, writes=[of_], dram_r=[self.OF])
            for kk in range(ndk):
                P.op("dve", lambda e, kk=kk: e.tensor_tensor_scan(
                    out=cum[:, kk, :], data0=rm, data1=lf[:, kk, :], initial=0.0, op0=ALU.mult, op1=ALU.add),
                    reads=[lf, self.masks], writes=[cum])
                if dirn == 0:
                    asrc = cum
                else:
                    P.op("pool", lambda e, kk=kk: e.tensor_tensor(out=arg[:, kk, :], in0=cum[:, kk, :],
                                                                   in1=lf[:, kk, :], op=ALU.subtract),
                         reads=[cum, lf], writes=[arg])
                    for c in range(nsub):
                        P.op("dve", lambda e, kk=kk, c=c: e.tensor_scalar(
                            out=arg[:, kk, c * C:(c + 1) * C], in0=arg[:, kk, c * C:(c + 1) * C], scalar1=-1.0,
                            scalar2=cum[:, kk, (c + 1) * C - 1:(c + 1) * C], op0=ALU.mult, op1=ALU.add),
                            reads=[arg, cum], writes=[arg])
                    asrc = arg
                P.op("act", lambda e, kk=kk, asrc=asrc: e.activation(out=Eq[:, kk, :], in_=asrc[:, kk, :], func=AF.Exp),
                     reads=[asrc], writes=[Eq])
                P.op("act", lambda e, kk=kk, asrc=asrc: e.activation(out=Ek[:, kk, :], in_=asrc[:, kk, :], func=AF.Exp,
                                                                      scale=-1.0), reads=[asrc], writes=[Ek])
                for c in range(nsub):
                    P.op("act", lambda e, kk=kk, c=c: e.activation(
                        out=el[:, kk, c:c + 1], in_=cum[:, kk, (c + 1) * C - 1:(c + 1) * C], func=AF.Exp),
                        reads=[cum], writes=[el])
                P.op("pool", lambda e, kk=kk: e.tensor_tensor(out=Eq[:, kk, :], in0=Eq[:, kk, :], in1=q[:, kk, :],
                                                               op=ALU.mult), reads=[Eq, q], writes=[Eq])
                P.op("dve", lambda e, kk=kk: e.tensor_tensor(out=Ek[:, kk, :], in0=Ek[:, kk, :], in1=k[:, kk, :],
                                                              op=ALU.mult), reads=[Ek, k], writes=[Ek])
            Qp, Kp = Eq, Ek
            pa = self.nps_a()
            for kk in range(ndk):
                P.op("pe", lambda e, kk=kk: e.matmul(pa[:, 0:128], lhsT=Kp[:, kk, :], rhs=Qp[:, kk, :],
                                                     start=(kk == 0), stop=(kk == ndk - 1)),
                     reads=[Kp, Qp], writes=[pa])
            at = att[g][n]
            mk_ = maskf if dirn == 0 else maskb
            P.op("dve", lambda e: e.tensor_tensor(out=at[:], in0=pa[:, 0:128], in1=mk_, op=ALU.mult),
                 reads=[pa, self.masks], writes=[at])
            po = self.nps_a()
            subs = list(range(nsub)) if dirn == 0 else list(range(nsub - 1, -1, -1))
            for c in subs:
                cs = slice(c * C, (c + 1) * C)
                for dvc in range(ndv):
                    ocol = slice(dvc * 128 + c * C, dvc * 128 + (c + 1) * C)
                    P.op("pe", lambda e, dvc=dvc, ocol=ocol, cs=cs: e.matmul(
                        po[:, ocol], lhsT=V1[:, dvc * 128:(dvc + 1) * 128], rhs=at[:, cs], start=True, stop=False),
                        reads=[V1, at], writes=[po])
                    for kk in range(ndk):
                        P.op("pe", lambda e, dvc=dvc, ocol=ocol, cs=cs, kk=kk: e.matmul(
                            po[:, ocol], lhsT=S[g][kk][:, dvc * 128:(dvc + 1) * 128], rhs=Qp[:, kk, cs],
                            start=False, stop=(kk == ndk - 1)), reads=[S[g][kk], Qp], writes=[po])
                pt = self.nps_b()
                for kk in range(ndk):
                    P.op("pe", lambda e, kk=kk, cs=cs, pt=pt: e.transpose(pt[0:C, kk * 128:(kk + 1) * 128],
                                                                           Kp[:, kk, cs], self.ident[:]),
                         reads=[Kp, self.ident], writes=[pt])
                kt = ktm[g][n]
                P.op("act", lambda e, pt=pt: e.activation(out=kt[0:C, :], in_=pt[0:C, 0:ndk * 128], func=AF.Copy),
                     reads=[pt], writes=[kt])
                for kk in range(ndk):
                    psS = self.nps_b()
                    if nsub > 1:
                        rhs_ = vsub[g][n][:, c, :]
                        rb = vsub[g][n]
                    else:
                        rhs_ = V1[:, :]
                        rb = V1
                    P.op("pe", lambda e, kk=kk, psS=psS, rhs_=rhs_: e.matmul(
                        psS[:, 0:dv], lhsT=kt[0:C, kk * 128:(kk + 1) * 128], rhs=rhs_, start=True, stop=True),
                        reads=[kt, rb], writes=[psS])
                    P.op("act", lambda e, kk=kk, c=c: e.activation(
                        out=S1[g][kk][:], in_=S[g][kk][:], func=AF.Copy, scale=el[:, kk, c:c + 1]),
                        reads=[S[g][kk], el], writes=[S1[g][kk]])
                    P.op("dve", lambda e, kk=kk, c=c, psS=psS: e.scalar_tensor_tensor(
                        out=S[g][kk][:], in0=psS[:, 0:dv], scalar=el[:, kk, c:c + 1], in1=S1[g][kk][:],
                        op0=ALU.mult, op1=ALU.add), reads=[psS, el, S1[g][kk]], writes=[S[g][kk]])
            ob = osb[g][n]
            if dirn == 0:
                P.op("act", lambda e: e.activation(out=ob[:], in_=po[:, 0:dv], func=AF.Copy), reads=[po], writes=[ob])
                P.dma("pool", self.OF.t[h * dv:(h + 1) * dv, t0:t0 + 128].rearrange("(k p) t -> p k t", p=128),
                      ob[:].rearrange("p (k t) -> p k t", k=ndv), ob, reads=[ob], dram_w=[self.OF])
            else:
                sg_, of_ = sgb[g][n], ofb[g][n]
                sq = sqb[g][n]
                rs = rsb[g][n]
                mx = mxb[g][n]
                P.op("dve", lambda e: e.tensor_tensor(out=ob[:], in0=po[:, 0:dv],
                                                      in1=of_[:].rearrange("p k t -> p (k t)"), op=ALU.add),
                     reads=[po, of_], writes=[ob])
                P.op("act", lambda e: e.activation(out=sq[:], in_=ob[:], func=AF.Square), reads=[ob], writes=[sq])
                pn = self.nps_a()
                for dvc in range(ndv):
                    P.op("pe", lambda e, dvc=dvc: e.matmul(pn[:, 0:128], lhsT=self.ones[:],
                                                           rhs=sq[:, dvc * 128:(dvc + 1) * 128],
                                                           start=(dvc == 0), stop=(dvc == ndv - 1)),
                         reads=[self.ones, sq], writes=[pn])
                P.op("act", lambda e: e.activation(out=rs[:], in_=pn[:, 0:128], func=AF.Sqrt,
                                                   bias=self.epsc[:, 0:1], scale=1.0 / dv),
                     reads=[pn, self.epsc], writes=[rs])
                P.op("dve", lambda e: e.reciprocal(out=rs[:], in_=rs[:]), reads=[rs], writes=[rs])
                for dvc in range(ndv):
                    P.op("pool", lambda e, dvc=dvc: e.tensor_tensor(
                        out=ob[:, dvc * 128:(dvc + 1) * 128], in0=ob[:, dvc * 128:(dvc + 1) * 128], in1=rs[:],
                        op=ALU.mult), reads=[ob, rs], writes=[ob])
                    P.op("dve", lambda e, dvc=dvc: e.scalar_tensor_tensor(
                        out=mx[:, dvc, :], in0=ob[:, dvc * 128:(dvc + 1) * 128], scalar=ong[:, dvc:dvc + 1],
                        in1=sg_[:, dvc, :], op0=ALU.mult, op1=ALU.mult), reads=[ob, ong, sg_], writes=[mx])
                r0 = cfg["mix0"] + h * dv
                P.dma("pool", self.MIXT.t[r0:r0 + dv, t0:t0 + 128].rearrange("(k p) t -> p k t", p=128),
                      mx[:], mx, reads=[mx], dram_w=[self.MIXT])

        for h0 in range(0, H, G):
            for dirn in range(2):
                for g in range(G):
                    for kk in range(ndk):
                        P.op("pool", lambda e, g=g, kk=kk: e.memset(S[g][kk][:], 0.0), writes=[S[g][kk]])
                if dirn == 1:
                    self.OF.fence()
                order = range(NBLK) if dirn == 0 else range(NBLK - 1, -1, -1)
                for bi in order:
                    for g in range(G):
                        block(g, h0 + g, dirn, bi)
        P.phase_reset()

    def phase_outproj(self, l):
        P = self.P
        T = self.T
        self.y.fence(); self.MIXT.fence()
        gb = self.make_gate_b(l, 1, 1.0)
        wo = P.sb("wo", [128, KC, D], BF16)
        P.dma("sp", wo[:], self.woutb[l].t.rearrange("(kc p) c -> p kc c", p=128), wo, writes=[wo],
              dram_r=[self.woutb[l]])
        mts = [P.sb("mt%d" % i, [128, KC, 512], BF16) for i in range(2)]
        xts = [P.sb("xt%d" % i, [128, D], F32) for i in range(2)]
        yts = [P.sb("yt%d" % i, [128, D], F32) for i in range(2)]
        n = 0
        for ti, t0 in enumerate(range(0, T, 512)):
            mt = mts[ti % 2]
            P.dma("sp", mt[:], self.MIXT.t[:, t0:t0 + 512].rearrange("(kc p) t -> p kc t", p=128), mt,
                  writes=[mt], dram_r=[self.MIXT])
            for si in range(4):
                xt = xts[n % 2]
                yt = yts[n % 2]
                n += 1
                r0 = t0 + si * 128
                P.dma("sp", xt[:], self.y.t[r0:r0 + 128, :], xt, writes=[xt], dram_r=[self.y])
                for nn in range(4):
                    pst = self.nps()
                    for kc in range(KC):
                        P.op("pe", lambda e, kc=kc, nn=nn, pst=pst, mt=mt, si=si: e.matmul(
                            pst[:], lhsT=mt[:, kc, si * 128:(si + 1) * 128], rhs=wo[:, kc, nn * 512:(nn + 1) * 512],
                            start=(kc == 0), stop=(kc == KC - 1)), reads=[mt, wo], writes=[pst])
                    cs = slice(nn * 512, (nn + 1) * 512)
                    P.op("dve", lambda e, pst=pst, yt=yt, cs=cs: e.tensor_tensor(
                        out=yt[:, cs], in0=pst[:], in1=gb[:, cs], op=ALU.mult), reads=[pst, gb], writes=[yt])
                    P.op("pool", lambda e, yt=yt, xt=xt, cs=cs: e.tensor_tensor(
                        out=yt[:, cs], in0=yt[:, cs], in1=xt[:, cs], op=ALU.add), reads=[yt, xt], writes=[yt])
                P.dma("pool", self.y.t[r0:r0 + 128, :], yt[:], yt, reads=[yt], dram_w=[self.y])
        P.phase_reset()

    def phase_inproj_ev(self, l):
        P = self.P
        T = self.T
        e_ = l // 2
        winb = self.winb[l]
        self.y.fence()
        for db in (self.UT, self.V, self.QN, self.QR, self.KN, self.KR, self.VA):
            db.fence()
        wuq = P.sb("wuq", [128, 4, 1536], BF16)
        wukv = P.sb("wukv", [128, 4, 2048], BF16)
        pswap = P.sb("pswap", [64, 64], F32)
        hT = P.sb("hT", [128, KC, 512], BF16)
        xts = [P.sb("xt%d" % i, [128, D], F32) for i in range(2)]
        xns = [P.sb("xn%d" % i, [128, D], F32) for i in range(2)]
        stats = [(P.sb("ss%d" % i, [128, 1], F32), P.sb("rstd%d" % i, [128, 1], F32)) for i in range(2)]
        wbl = [P.sb("wbl%d" % i, [128, KC, 512], BF16) for i in range(2)]
        sts = [P.sb("st%d" % i, [128, 512], F32) for i in range(5)]
        sbs = [P.sb("sb%d" % i, [128, 512], BF16) for i in range(3)]
        tmp = [P.sb("tmp%d" % i, [128, 512], F32) for i in range(3)]
        oml = P.sb("oml", [128, 16], F32)
        mrow = P.sb("mrow", [128, 512], F32)
        cq = P.sb("cq", [128, 4, 512], F32)
        ckv = P.sb("ckv", [128, 4, 512], F32)
        sq4 = P.sb("sq4", [128, 4, 512], F32)
        cqn = P.sb("cqn", [128, 4, 512], BF16)
        ckvn = P.sb("ckvn", [128, 4, 512], BF16)
        kpe = P.sb("kpe", [64, 512], F32)
        sqk = P.sb("sqk", [128, 512], F32)
        sqrq = P.sb("sqrq", [128, 512], F32)
        skp = P.sb("skp", [128, 512], F32)
        P.op("pool", lambda e: e.memset(sqk[:], 0.0), writes=[sqk])
        P.op("pool", lambda e: e.memset(sqrq[:], 0.0), writes=[sqrq])
        krot = P.sb("krot", [64, 512], F32)
        rs = P.sb("rs", [128, 512], F32)
        rope = P.sb("rope", [64, 2, 512], F32)
        gq = P.sb("gq", [128, 12], F32)
        P.dma("sp", wuq[:], self.wuqb[l].t.rearrange("(c p) n -> p c n", p=128), wuq, writes=[wuq],
              dram_r=[self.wuqb[l]])
        P.dma("sp", wukv[:], self.wukvb[l].t.rearrange("(c p) n -> p c n", p=128), wukv, writes=[wukv],
              dram_r=[self.wukvb[l]])
        P.dma("sp", pswap[:], self.pswap_d[:, :], pswap, writes=[pswap])
        P.dma("sp", gq[:, 0:4], self.mla_qa_col[e_], gq, writes=[gq])
        P.dma("sp", gq[:, 4:8], self.mla_kva_col[e_], gq, writes=[gq])
        P.dma("sp", gq[:, 8:10], self.mla_qn_col[e_], gq, writes=[gq])
        P.dma("sp", gq[:, 10:12], self.mla_kn_col[e_], gq, writes=[gq])
        P.op("dve", lambda e: e.tensor_scalar(out=gq[:, 8:10], in0=gq[:, 8:10], scalar1=float(192 ** -0.5),
                                               scalar2=None, op0=ALU.mult), reads=[gq], writes=[gq])
        if e_ == 0:
            P.op("dve", lambda e: e.memset(oml[:], 1.0), writes=[oml])
        else:
            lbr = P.sb("lbr", [128, 2, 2, 8], F32)
            P.dma("sp", lbr[:], self.hg_lb_col.rearrange("d e p k -> p d e k"), lbr, writes=[lbr])
            for d_ in range(2):
                P.op("dve", lambda e, d_=d_: e.tensor_tensor(out=oml[:, d_ * 8:(d_ + 1) * 8], in0=lbr[:, d_, 0, :],
                                                             in1=lbr[:, d_, 1, :], op=ALU.subtract),
                     reads=[lbr], writes=[oml])
            P.op("act", lambda e: e.activation(out=oml[:], in_=oml[:], func=AF.Sigmoid), reads=[oml], writes=[oml])
        cn = {"w": 0, "st": 0, "sb": 0, "tmp": 0, "x": 0}

        def nxt(lst, key):
            b_ = lst[cn[key] % len(lst)]
            cn[key] += 1
            return b_

        def rms_bcast(pn, n):
            P.op("act", lambda e: e.activation(out=rs[:], in_=pn[:], func=AF.Sqrt, bias=self.epsc[:, 0:1],
                                               scale=1.0 / n), reads=[pn, self.epsc], writes=[rs])
            P.op("dve", lambda e: e.reciprocal(out=rs[:], in_=rs[:]), reads=[rs], writes=[rs])

        def norm512(src, g0, dst):
            P.op("act", lambda e: e.activation(out=sq4[:], in_=src[:], func=AF.Square), reads=[src], writes=[sq4])
            pn = self.nps()
            for c in range(4):
                P.op("pe", lambda e, c=c: e.matmul(pn[:], lhsT=self.ones[:], rhs=sq4[:, c, :], start=(c == 0),
                                                   stop=(c == 3)), reads=[self.ones, sq4], writes=[pn])
            rms_bcast(pn, 512)
            for c in range(4):
                P.op("dve", lambda e, c=c: e.scalar_tensor_tensor(
                    out=dst[:, c, :], in0=src[:, c, :], scalar=gq[:, g0 + c:g0 + c + 1], in1=rs[:],
                    op0=ALU.mult, op1=ALU.mult), reads=[src, gq, rs], writes=[dst])

        def rope_apply(xr, out):
            pw = self.nps()
            P.op("pe", lambda e: e.matmul(pw[0:64, :], lhsT=pswap[:, :], rhs=xr[0:64, :], start=True, stop=True),
                 reads=[pswap, xr], writes=[pw])
            t2 = nxt(tmp, "tmp")
            P.op("dve", lambda e: e.tensor_tensor(out=t2[0:64, :], in0=pw[0:64, :], in1=rope[:, 1, :], op=ALU.mult),
                 reads=[pw, rope], writes=[t2])
            P.op("pool", lambda e: e.tensor_tensor(out=out[0:64, :], in0=xr[0:64, :], in1=rope[:, 0, :], op=ALU.mult),
                 reads=[xr, rope], writes=[out])
            P.op("pool", lambda e: e.tensor_tensor(out=out[0:64, :], in0=out[0:64, :], in1=t2[0:64, :], op=ALU.add),
                 reads=[out, t2], writes=[out])

        cnt = 0
        for t0 in range(0, T, 512):
            self.load_norm_tile(l, t0, hT, xts, xns, stats, cnt)
            cnt += 4
            P.dma("sp", rope[:], self.rope_d[:, :, t0:t0 + 512].rearrange("a p t -> p a t"), rope, writes=[rope])
            P.dma("sp", mrow[:], self.tmask_row_d[0:1, t0:t0 + 512].partition_broadcast(128), mrow, writes=[mrow])
            for b in range(13):
                wb = nxt(wbl, "w")
                ncol = 512 if b < 12 else 64
                P.dma("sp", wb[:, :, 0:ncol], winb.t[:, b * 512:b * 512 + ncol].rearrange(
                    "(kc p) c -> p kc c", p=128), wb, writes=[wb], dram_r=[winb])
                if b < 2 or b in (8, 9):
                    for cc in range(4):
                        pst = self.nps()
                        self.fm_mm(pst, wb, cc * 128, 128, hT)
                        st = nxt(sts, "st")
                        P.op("act", lambda e, st=st, pst=pst: e.activation(out=st[:], in_=pst[:], func=AF.Silu),
                             reads=[pst], writes=[st])
                        row0 = (b * 512 if b < 2 else 5120 + (b - 8) * 512) + cc * 128
                        self.store_fm(st, row0, t0)
                elif b < 6:
                    d_ = (b - 2) // 2
                    for cc in range(4):
                        ch = ((b - 2) % 2) * 4 + cc
                        pst = self.nps()
                        self.fm_mm(pst, wb, cc * 128, 128, hT)
                        t1 = nxt(tmp, "tmp")
                        P.op("act", lambda e, t1=t1, pst=pst: e.activation(out=t1[:], in_=pst[:], func=AF.Sigmoid,
                                                                           scale=-1.0), reads=[pst], writes=[t1])
                        st = nxt(sts, "st")
                        P.op("dve", lambda e, st=st, t1=t1, d_=d_, ch=ch: e.scalar_tensor_tensor(
                            out=st[:], in0=t1[:], scalar=oml[:, d_ * 8 + ch:d_ * 8 + ch + 1], in1=mrow[:],
                            op0=ALU.mult, op1=ALU.mult), reads=[t1, oml, mrow], writes=[st])
                        self.store_fm(st, 1024 + d_ * 1024 + ch * 128, t0)
                        st2 = nxt(sts, "st")
                        P.op("act", lambda e, st=st, st2=st2: e.activation(out=st2[:], in_=st[:], func=AF.Ln, scale=-1.0,
                                                                           bias=self.onec[:, 0:1]),
                             reads=[st, self.onec], writes=[st2])
                        self.store_fm(st2, 3072 + d_ * 1024 + ch * 128, t0)
                elif b < 8:
                    for si in range(4):
                        pst = self.nps()
                        self.tm_mm(pst, wb, si, hT)
                        st = nxt(sts, "st")
                        ti = (t0 // 128) + si
                        P.op("act", lambda e, st=st, pst=pst, ti=ti: e.activation(
                            out=st[:], in_=pst[:], func=AF.Copy, scale=self.tmask[:, ti:ti + 1]),
                            reads=[pst, self.tmask], writes=[st])
                        P.dma("pool", self.V.t[t0 + si * 128:t0 + (si + 1) * 128, (b - 6) * 512:(b - 5) * 512],
                              st[:], st, reads=[st], dram_w=[self.V])
                elif b < 12:
                    dst = cq if b == 10 else ckv
                    for cc in range(4):
                        pst = self.nps()
                        self.fm_mm(pst, wb, cc * 128, 128, hT)
                        P.op("dve", lambda e, pst=pst, cc=cc, dst=dst: e.tensor_copy(out=dst[:, cc, :], in_=pst[:]),
                             reads=[pst], writes=[dst])
                else:
                    pst = self.nps()
                    self.fm_mm(pst, wb, 0, 64, hT)
                    P.op("dve", lambda e, pst=pst: e.tensor_copy(out=kpe[:], in_=pst[0:64, :]), reads=[pst], writes=[kpe])
            import os
            kev = os.environ.get("K_EV", "")
            if "nomla" in kev:
                continue
            norm512(cq, 0, cqn)
            norm512(ckv, 4, ckvn)
            P.op("act", lambda e: e.activation(out=sqk[0:64, :], in_=kpe[:], func=AF.Square), reads=[kpe], writes=[sqk])
            pk2 = self.nps()
            sqk_src = sqk if "dbgsq" not in kev else sq4
            P.op("pe", lambda e, pk2=pk2, sqk_src=sqk_src: e.matmul(pk2[:], lhsT=self.ones[:], rhs=sqk_src[:, 0:512] if sqk_src is sqk else sqk_src[:, 0, :], start=True, stop=True),
                 reads=[self.ones, sqk_src], writes=[pk2])
            P.op("act", lambda e, pk2=pk2: e.activation(out=skp[:], in_=pk2[:], func=AF.Copy), reads=[pk2], writes=[skp])
            kg = nxt(tmp, "tmp")
            P.op("dve", lambda e, kg=kg: e.tensor_scalar(out=kg[0:64, :], in0=kpe[:], scalar1=gq[0:64, 11:12],
                                                         scalar2=None, op0=ALU.mult), reads=[kpe, gq], writes=[kg])
            rope_apply(kg, krot)
            for h in range(int(os.environ.get("K_NH", "8")) if "noqk" not in kev else 0):
                for which in range(2):
                    if ("noq" in kev and which == 0) or ("nok" in kev and which == 1):
                        continue
                    w_ = wuq if which == 0 else wukv
                    src = cqn if which == 0 else ckvn
                    c0 = h * 192 if which == 0 else h * 256
                    gcol = 8 if which == 0 else 10
                    pq = self.nps()
                    for c in range(4):
                        P.op("pe", lambda e, c=c, w_=w_, src=src, c0=c0, pq=pq: e.matmul(
                            pq[:], lhsT=w_[:, c, c0:c0 + 128], rhs=src[:, c, :], start=(c == 0), stop=(c == 3)),
                            reads=[w_, src], writes=[pq])
                    if int(os.environ.get("K_STOP", "9")) <= 0:
                        continue
                    nf = nxt(tmp, "tmp")
                    sqn = nxt(tmp, "tmp")
                    P.op("dve", lambda e, nf=nf, pq=pq: e.tensor_copy(out=nf[:], in_=pq[:]), reads=[pq], writes=[nf])
                    P.op("act", lambda e, sqn=sqn, nf=nf: e.activation(out=sqn[:], in_=nf[:], func=AF.Square),
                         reads=[nf], writes=[sqn])
                    kstop = int(os.environ.get("K_STOP", "9"))
                    if kstop <= 1:
                        continue
                    if which == 0:
                        pr = self.nps()
                        for c in range(4):
                            P.op("pe", lambda e, c=c, c0=c0, pr=pr: e.matmul(
                                pr[0:64, :], lhsT=wuq[:, c, c0 + 128:c0 + 192], rhs=cqn[:, c, :], start=(c == 0),
                                stop=(c == 3)), reads=[wuq, cqn], writes=[pr])
                        qr = nxt(sts, "st")
                        sqr = sqrq
                        P.op("dve", lambda e, qr=qr, pr=pr: e.tensor_copy(out=qr[0:64, :], in_=pr[0:64, :]),
                             reads=[pr], writes=[qr])
                        P.op("act", lambda e, sqr=sqr, qr=qr: e.activation(out=sqr[0:64, :], in_=qr[0:64, :],
                                                                           func=AF.Square), reads=[qr], writes=[sqr])
                        P.op("dve", lambda e, qr=qr: e.tensor_scalar(
                            out=qr[0:64, :], in0=qr[0:64, :], scalar1=gq[0:64, 9:10], scalar2=None, op0=ALU.mult),
                            reads=[qr, gq], writes=[qr])
                        sq_r = sqr
                    else:
                        sq_r = sqk
                    pn = self.nps()
                    P.op("pe", lambda e, pn=pn, sqn=sqn: e.matmul(pn[:], lhsT=self.ones[:], rhs=sqn[:], start=True,
                                                                  stop=True), reads=[self.ones, sqn], writes=[pn])
                    if which == 0:
                        pn2 = self.nps()
                        P.op("pe", lambda e, pn2=pn2: e.matmul(pn2[:], lhsT=self.ones[:], rhs=sqrq[:], start=True,
                                                               stop=True), reads=[self.ones, sqrq], writes=[pn2])
                        add_sb = nxt(sts, "st")
                        P.op("act", lambda e, pn2=pn2, add_sb=add_sb: e.activation(out=add_sb[:], in_=pn2[:],
                                                                                   func=AF.Copy),
                             reads=[pn2], writes=[add_sb])
                    else:
                        add_sb = skp
                    ssum = nxt(sts, "st")
                    P.op("dve", lambda e, pn=pn, add_sb=add_sb, ssum=ssum: e.tensor_tensor(
                        out=ssum[:], in0=pn[:], in1=add_sb[:], op=ALU.add), reads=[pn, add_sb], writes=[ssum])
                    if "noadd" not in kev:
                        pn = ssum
                    if kstop <= 2:
                        continue
                    rms_bcast(pn, 192)
                    if kstop <= 3:
                        continue
                    ob = nxt(sbs, "sb")
                    P.op("dve", lambda e, ob=ob, nf=nf, gcol=gcol: e.scalar_tensor_tensor(
                        out=ob[:], in0=nf[:], scalar=gq[:, gcol:gcol + 1], in1=rs[:], op0=ALU.mult, op1=ALU.mult),
                        reads=[nf, gq, rs], writes=[ob])
                    dn = self.QN if which == 0 else self.KN
                    if "nodma" not in kev:
                        P.dma("pool", dn.t[h, :, t0:t0 + 512], ob[:], ob, reads=[ob], dram_w=[dn])
                    if kstop <= 4:
                        continue
                    ob2 = nxt(sbs, "sb")
                    if which == 0:
                        qrot = nxt(tmp, "tmp")
                        rope_apply(qr, qrot)
                        rsrc = qrot
                    else:
                        rsrc = krot
                    P.op("dve", lambda e, ob2=ob2, rsrc=rsrc: e.tensor_tensor(
                        out=ob2[0:64, :], in0=rsrc[0:64, :], in1=rs[0:64, :], op=ALU.mult),
                        reads=[rsrc, rs], writes=[ob2])
                    dr = self.QR if which == 0 else self.KR
                    if "nodma" not in kev:
                        P.dma("pool", dr.t[h, :, t0:t0 + 512], ob2[0:64, :], ob2, reads=[ob2], dram_w=[dr])
            wv = wukv[:].rearrange("p c (h x) -> p c h x", x=256)
            for si in range(4 if "nov" not in kev else 0):
                for half in range(2):
                    pv = self.nps()
                    for c in range(4):
                        P.op("pe", lambda e, c=c, si=si, half=half, pv=pv: e.matmul(
                            pv[:].rearrange("p (h x) -> p h x", x=128), lhsT=ckvn[:, c, si * 128:(si + 1) * 128],
                            rhs=wv[:, c, half * 4:(half + 1) * 4, 128:256], start=(c == 0), stop=(c == 3)),
                            reads=[ckvn, wukv], writes=[pv])
                    ob = nxt(sbs, "sb")
                    P.op("act", lambda e, ob=ob, pv=pv: e.activation(out=ob[:], in_=pv[:], func=AF.Copy),
                         reads=[pv], writes=[ob])
                    P.dma("pool", self.VA.t[t0 + si * 128:t0 + (si + 1) * 128, half * 512:(half + 1) * 512], ob[:], ob,
                          reads=[ob], dram_w=[self.VA])
        P.phase_reset()

    def phase_attn(self, l):
        P = self.P
        T = self.T
        NB = T // 128
        for db in (self.QN, self.QR, self.KN, self.KR, self.VA, self.MIXT):
            db.fence()
        onesb = P.sb("onesb", [128, 128], BF16)
        P.op("dve", lambda e: e.memset(onesb[:], 1.0), writes=[onesb])
        kn = [P.sb("kn%d" % i, [128, T], BF16) for i in range(2)]
        kr = [P.sb("kr%d" % i, [128, T], BF16) for i in range(2)]
        va = [P.sb("va%d" % i, [128, NB, 128], BF16) for i in range(2)]
        qn = [P.sb("qn%d" % i, [128, 512], BF16) for i in range(2)]
        qr = [P.sb("qr%d" % i, [128, 512], BF16) for i in range(2)]
        for i in range(2):
            P.op("pool", lambda e, i=i: e.memset(kr[i][64:128, :], 0.0), writes=[kr[i]])
            P.op("pool", lambda e, i=i: e.memset(qr[i][64:128, :], 0.0), writes=[qr[i]])
        pts = [P.sb("pt%d" % i, [128, 512], BF16) for i in range(3)]
        rd = [P.sb("rd%d" % i, [128, 512], F32) for i in range(2)]
        mx = [P.sb("mx%d" % i, [128, 512], BF16) for i in range(2)]
        nq = 0
        npt = 0
        for h in range(8):
            K1, K2, V1 = kn[h % 2], kr[h % 2], va[h % 2]
            P.dma("sp", K1[:], self.KN.t[h], K1, writes=[K1], dram_r=[self.KN])
            P.dma("sp", K2[0:64, :], self.KR.t[h], K2, writes=[K2], dram_r=[self.KR])
            P.dma("sp", V1[:], self.VA.t[:, h * 128:(h + 1) * 128].rearrange("(b p) d -> p b d", p=128), V1,
                  writes=[V1], dram_r=[self.VA])
            for t0 in range(0, T, 512):
                Q1, Q2 = qn[nq % 2], qr[nq % 2]
                po, pd = self.ps[(nq % 2) * 2], self.ps[(nq % 2) * 2 + 1]
                rdt, mxt = rd[nq % 2], mx[nq % 2]
                nq += 1
                P.dma("sp", Q1[:], self.QN.t[h, :, t0:t0 + 512], Q1, writes=[Q1], dram_r=[self.QN])
                P.dma("sp", Q2[0:64, :], self.QR.t[h, :, t0:t0 + 512], Q2, writes=[Q2], dram_r=[self.QR])
                for jb in range(NB):
                    psc = self.ps[4 + npt % 4]
                    pt = pts[npt % 3]
                    npt += 1
                    js = slice(jb * 128, (jb + 1) * 128)
                    P.op("pe", lambda e, psc=psc, js=js, K1=K1, Q1=Q1: e.matmul(
                        psc[:], lhsT=K1[:, js], rhs=Q1[:], start=True, stop=False), reads=[K1, Q1], writes=[psc])
                    P.op("pe", lambda e, psc=psc, js=js, K2=K2, Q2=Q2: e.matmul(
                        psc[:], lhsT=K2[:, js], rhs=Q2[:, :], start=False, stop=True), reads=[K2, Q2], writes=[psc])
                    P.op("act", lambda e, psc=psc, pt=pt, jb=jb: e.activation(
                        out=pt[:], in_=psc[:], func=AF.Exp, bias=self.kbias[:, jb:jb + 1]),
                        reads=[psc, self.kbias], writes=[pt])
                    P.op("pe", lambda e, po=po, pt=pt, jb=jb, V1=V1: e.matmul(
                        po[:], lhsT=V1[:, jb, :], rhs=pt[:], start=(jb == 0), stop=(jb == NB - 1)),
                        reads=[V1, pt], writes=[po])
                    P.op("pe", lambda e, pd=pd, pt=pt, jb=jb: e.matmul(
                        pd[:], lhsT=onesb[:], rhs=pt[:], start=(jb == 0), stop=(jb == NB - 1)),
                        reads=[onesb, pt], writes=[pd])
                P.op("dve", lambda e, rdt=rdt, pd=pd: e.reciprocal(out=rdt[:], in_=pd[:]), reads=[pd], writes=[rdt])
                P.op("dve", lambda e, rdt=rdt, po=po, mxt=mxt: e.tensor_tensor(out=mxt[:], in0=po[:], in1=rdt[:],
                                                                               op=ALU.mult),
                     reads=[po, rdt], writes=[mxt])
                P.dma("pool", self.MIXT.t[1024 + h * 128:1024 + (h + 1) * 128, t0:t0 + 512], mxt[:], mxt, reads=[mxt],
                      dram_w=[self.MIXT])
        P.phase_reset()


def col_layout(v):
    n = v.shape[-1] // 128
    return np.ascontiguousarray(np.swapaxes(v.reshape(v.shape[:-1] + (n, 128)), -1, -2))


def _const_masks():
    j = np.arange(128)[:, None]
    i = np.arange(128)[None, :]
    t = np.arange(128)[None, :] + 0 * j
    out = np.zeros((6, 128, 128), np.float32)
    out[0] = (j <= i)
    out[1] = (j >= i)
    out[2] = (t % 128 != 0)
    out[3] = (j <= i) & (j // 32 == i // 32)
    out[4] = (j >= i) & (j // 32 == i // 32)
    out[5] = (t % 32 != 0)
    return out


CONST_MASKS = _const_masks()
PSWAP = np.zeros((64, 64), np.float32)
for _m in range(64):
    PSWAP[(_m + 32) % 64, _m] = 1.0


def rope_tables(T):
    half = 32
    inv_freq = (10000.0 ** (-np.arange(half, dtype=np.float32) / half)).astype(np.float32)
    ang = np.arange(T, dtype=np.float32)[None, :] * inv_freq[:, None]
    cos, sin = np.cos(ang).astype(np.float32), np.sin(ang).astype(np.float32)
    tab = np.zeros((2, 64, T), np.float32)
    tab[0, :32] = cos
    tab[0, 32:] = cos
    tab[1, :32] = -sin
    tab[1, 32:] = sin
    return tab


def make_in_maps(inputs, T, ncores, layers=tuple(range(DEPTH))):
    xp, xs = inputs["x_prompt"], inputs["x_sample"]
    maps = []
    for c in range(ncores):
        if c < 4:
            x = xp[c]
            cc = inputs["c_prompt"][c]
        else:
            x = np.zeros((T, D), np.float32)
            x[: xs.shape[1]] = xs[c - 4]
            cc = inputs["c_sample"][c - 4]
        m = {
            "x": np.ascontiguousarray(x[:T]),
            "c_col": col_layout(cc),
            "ada_b_col": col_layout(inputs["ada_b"]),
            "norm_g_col": col_layout(inputs["norm_g"]).transpose(0, 2, 1, 3).reshape(DEPTH, 128, 3 * KC),
            "ident": np.eye(128, dtype=np.float32),
        }
        for l in layers:
            m["ada_w_%d" % l] = inputs["ada_w"][l]
            m["ffn_w13_%d" % l] = inputs["ffn_w13"][l]
            m["ffn_w2_%d" % l] = inputs["ffn_w2"][l]
        nreal = T if c < 4 else min(T, xs.shape[1])
        tm = (np.arange(T) < nreal).astype(np.float32)
        m["tmask_col"] = col_layout(tm)
        m["tmask_row"] = tm.reshape(1, T)
        m["kbias_col"] = col_layout(np.where(np.arange(T) < nreal, 0.0, -30000.0).astype(np.float32))
        m["masks"] = CONST_MASKS
        m["od_w_in"] = inputs["od_w_in"]
        m["od_w_out"] = inputs["od_w_out"]
        m["gla_gk_w2"] = inputs["gla_gk_w2"]
        m["gla_gk_b_col"] = col_layout(inputs["gla_gk_b"])
        m["gla_onorm_g_col"] = col_layout(inputs["gla_onorm_g"])
        m["ev_w_in"] = inputs["ev_w_in"]
        m["ev_w_out"] = inputs["ev_w_out"]
        m["hgrn_lb_col"] = col_layout(inputs["hgrn_lb"])
        m["hgrn_onorm_g_col"] = col_layout(inputs["hgrn_onorm_g"])
        m["mla_qa_norm_g_col"] = col_layout(inputs["mla_qa_norm_g"])
        m["mla_kva_norm_g_col"] = col_layout(inputs["mla_kva_norm_g"])
        m["mla_w_uq"] = inputs["mla_w_uq"]
        m["mla_w_ukv"] = inputs["mla_w_ukv"]
        m["mla_qn_g_col"] = col_layout(np.pad(inputs["mla_qn_g"], ((0, 0), (0, 64))))
        m["mla_kn_g_col"] = col_layout(np.pad(inputs["mla_kn_g"], ((0, 0), (0, 64))))
        m["rope_tab"] = rope_tables(T)
        m["pswap"] = PSWAP
        maps.append(m)
    return maps


def kernel(**inputs):
    inputs = {k: np.asarray(v) for k, v in inputs.items()}
    T = inputs["x_prompt"].shape[1]
    b = Builder(T, list(range(DEPTH)))
    maps = make_in_maps(inputs, T, 8)
    res = run_bass_kernel_spmd(b.nc, maps, core_ids=list(range(8)))
    yp = np.stack([res.results[c]["y"] for c in range(4)], 0)
    Ts = inputs["x_sample"].shape[1]
    ys = np.stack([res.results[c]["y"][:Ts] for c in range(4, 8)], 0)
    return (yp.astype(np.float32), ys.astype(np.float32))
```

```python
import numpy as np
import concourse.bass as bass
import concourse.mybir as mybir
from concourse.bass_utils import run_bass_kernel_spmd

F32 = mybir.dt.float32
BF16 = mybir.dt.bfloat16
AF = mybir.ActivationFunctionType
ALU = mybir.AluOpType

D = 2048
FF = 5632
NFC = FF // 128
KC = D // 128
EPS = 1e-6
DEPTH = 4

ENGS = ["pe", "act", "dve", "pool", "sp"]


class Buf:
    __slots__ = ("name", "t", "w", "r", "sem", "cnt", "q", "nobar")

    def __init__(self, name, t=None):
        self.q = None
        self.nobar = False
        self.name = name
        self.t = t
        self.w = None
        self.r = []
        self.sem = None
        self.cnt = 0

    def __getitem__(self, idx):
        return self.t[idx]


class DramBuf:
    __slots__ = ("name", "t", "pending", "fenced")

    def __init__(self, name, t):
        self.name = name
        self.t = t
        self.pending = {}
        self.fenced = {}

    def __getitem__(self, idx):
        return self.t[idx]

    def fence(self):
        for k, v in self.pending.items():
            if self.fenced.get(k, (None, 0))[1] < v[1]:
                self.fenced[k] = v
        self.pending = {}


class Rec:
    __slots__ = ("eng", "fn", "deps", "needed", "tick", "dma")

    def __init__(self, eng, fn, deps, dma=None):
        self.eng = eng
        self.fn = fn
        self.deps = deps
        self.needed = False
        self.tick = 0
        self.dma = dma


class Prog:
    def __init__(self, nc):
        self.nc = nc
        self.ops = {e: [] for e in ENGS}
        self.last = {e: None for e in ENGS}
        self.bufs = []
        self.nsem = 0
        self.sb_off = 20480
        self.sb_base = 20480
        self.nalloc = 0
        self.sempool = {"sp": [], "pool": []}
        self.nbase = 0
        self.engsem = {e: nc.alloc_semaphore(name="prog_" + e) for e in ENGS if e != "sp"}

    def sb(self, name, shape, dtype, dma=False):
        esz = 4 if dtype == F32 else 2
        n = 1
        for s_ in shape[1:]:
            n *= s_
        nbytes = (n * esz + 63) // 64 * 64
        self.nalloc += 1
        t = self.nc.alloc_sbuf_tensor_at(
            "%s_%d" % (name, self.nalloc), list(shape), dtype, offset=self.sb_off
        )
        self.sb_off += nbytes
        assert self.sb_off <= 229376, ("SBUF overflow", name, self.sb_off)
        b = Buf(name, t)
        self.bufs.append(b)
        return b

    def _bufsem(self, b, q):
        if b.sem is None:
            b.q = q
            if self.sempool[q]:
                b.sem, b.cnt = self.sempool[q].pop()
            else:
                b.sem = self.nc.alloc_semaphore(name="bs%d_%s" % (self.nsem, b.name))
                self.nsem += 1
        assert b.q == q, ("buffer DMA'd from two queue kinds", b.name)
        return b.sem

    def phase_mark(self):
        self.sb_base = self.sb_off
        self.nbase = len(self.bufs)

    def phase_reset(self):
        self.barrier()
        self.sb_off = self.sb_base
        for b in self.bufs[self.nbase:]:
            if b.sem is not None:
                self.sempool[b.q].append((b.sem, b.cnt))
        del self.bufs[self.nbase:]

    def _collect(self, eng, reads, writes, compute=False):
        deps = []
        for b in reads:
            if b.w is not None:
                deps.append(b.w)
        for b in writes:
            if b.w is not None:
                deps.append(b.w)
            deps.extend(b.r)
        out = []
        for d in deps:
            if isinstance(d, Rec):
                if d.eng == eng and (eng == "pe" or (compute and eng in ("act", "dve"))):
                    continue
                d.needed = True
            out.append(d)
        return out

    def op(self, eng, fn, reads=(), writes=()):
        deps = self._collect(eng, reads, writes, compute=True)
        r = Rec(eng, fn, deps)
        self.ops[eng].append(r)
        self.last[eng] = r
        for b in reads:
            b.r.append(r)
        for b in writes:
            b.w = r
            b.r = []
        return r

    def dma(self, q, out, in_, sembuf, reads=(), writes=(), dram_r=(), dram_w=()):
        deps = self._collect(q, reads, writes)
        for db in list(dram_r) + list(dram_w):
            deps.extend(db.fenced.values())
        sem = self._bufsem(sembuf, q)
        sembuf.cnt += 1
        ev = (sem, 16 * sembuf.cnt)
        r = Rec(q, lambda e: e.dma_start(out=out, in_=in_), deps, dma=ev)
        self.ops[q].append(r)
        for b in reads:
            b.r.append(ev)
        for b in writes:
            b.w = ev
            b.r = []
        for db in dram_w:
            db.pending[id(sem)] = ev
        return r

    def barrier(self):
        evs = []
        for e in ENGS:
            if e != "sp" and self.last[e] is not None:
                self.last[e].needed = True
                evs.append(self.last[e])
        for b in self.bufs:
            if b.sem is not None and b.cnt > 0 and not b.nobar:
                evs.append((b.sem, 16 * b.cnt))
        for e in ENGS:
            deps = [d for d in evs if not (isinstance(d, Rec) and d.eng == e)]
            self.ops[e].append(Rec(e, None, deps))
        for b in self.bufs:
            b.w = None
            b.r = []

    def emit(self):
        nc = self.nc
        for e in ENGS:
            t = 0
            for r in self.ops[e]:
                if r.dma is None and r.needed:
                    t += 1
                    r.tick = t
        engsem = self.engsem
        ops = self.ops

        def run(e, h):
            seen = {}
            for r in ops[e]:
                waits = {}
                for d in r.deps:
                    if isinstance(d, Rec):
                        sem, val = engsem[d.eng], d.tick
                    else:
                        sem, val = d
                    k = id(sem)
                    if seen.get(k, 0) >= val:
                        continue
                    if k not in waits or waits[k][1] < val:
                        waits[k] = (sem, val)
                for k, (sem, val) in waits.items():
                    h.wait_ge(sem, val)
                    seen[k] = val
                if r.fn is None:
                    continue
                ins = r.fn(h)
                if r.dma is not None:
                    ins.then_inc(r.dma[0], 16)
                elif r.needed:
                    ins.then_inc(engsem[e], 1)

        with nc.Block() as block:

            @block.tensor
            def _(h):
                run("pe", h)

            @block.scalar
            def _(h):
                run("act", h)

            @block.vector
            def _(h):
                run("dve", h)

            @block.gpsimd
            def _(h):
                run("pool", h)

            @block.sync
            def _(h):
                run("sp", h)


class Builder:
    def __init__(self, T, layers, ffn_only=False, dbg=None, TT=512):
        self.T = T
        self.layers = layers
        self.TT = TT
        self.ffn_only = ffn_only
        nc = bass.Bass("TRN2", target_bir_lowering=False)
        self.nc = nc
        self.P = Prog(nc)
        P = self.P
        L = DEPTH

        def din(name, shape, dt=F32):
            return nc.dram_tensor(name, list(shape), dt, kind="ExternalInput").ap()

        self.x = din("x", [T, D])
        self.y = DramBuf("y", nc.dram_tensor("y", [T, D], F32, kind="ExternalOutput").ap())
        self.c_col = din("c_col", [128, KC])
        self.ada_w = {l: din("ada_w_%d" % l, [D, 9 * D]) for l in layers}
        self.ada_b_col = din("ada_b_col", [L, 128, 144])
        self.norm_g_col = din("norm_g_col", [L, 128, 3 * KC])
        self.ffn_w13 = {l: din("ffn_w13_%d" % l, [2, D, 2 * FF]) for l in layers}
        self.ffn_w2 = {l: din("ffn_w2_%d" % l, [2, FF, D]) for l in layers}
        self.ident_d = din("ident", [128, 128])
        NB = T // 128
        self.tmask_d = din("tmask_col", [128, NB])
        self.kbias_d = din("kbias_col", [128, NB])
        self.tmask_row_d = din("tmask_row", [1, T])
        self.masks_d = din("masks", [6, 128, 128])
        self.od_w_in = din("od_w_in", [2, D, 6176])
        self.od_w_out = din("od_w_out", [2, D, D])
        self.gla_w2 = din("gla_gk_w2", [2, 2, 16, 1024])
        self.gla_b_col = din("gla_gk_b_col", [2, 2, 128, 8])
        self.gla_ong_col = din("gla_onorm_g_col", [2, 128, 4])
        self.ev_w_in = din("ev_w_in", [2, D, 6208])
        self.ev_w_out = din("ev_w_out", [2, D, D])
        self.hg_lb_col = din("hgrn_lb_col", [2, 2, 128, 8])
        self.hg_ong_col = din("hgrn_onorm_g_col", [2, 128, 1])
        self.mla_qa_col = din("mla_qa_norm_g_col", [2, 128, 4])
        self.mla_kva_col = din("mla_kva_norm_g_col", [2, 128, 4])
        self.mla_w_uq = din("mla_w_uq", [2, 512, 1536])
        self.mla_w_ukv = din("mla_w_ukv", [2, 512, 2048])
        self.mla_qn_col = din("mla_qn_g_col", [2, 128, 2])
        self.mla_kn_col = din("mla_kn_g_col", [2, 128, 2])
        self.rope_d = din("rope_tab", [2, 64, T])
        self.pswap_d = din("pswap", [64, 64])
        self.winb = {}
        self.woutb = {}
        for l in layers:
            nin = 6208 if l % 2 == 0 else 6176
            self.winb[l] = DramBuf("winb", nc.dram_tensor("winb_%d" % l, [D, nin], BF16).ap())
            self.woutb[l] = DramBuf("woutb", nc.dram_tensor("woutb_%d" % l, [D, D], BF16).ap())
        self.wuqb = {}
        self.wukvb = {}
        for l in layers:
            if l % 2 == 0:
                self.wuqb[l] = DramBuf("wuqb", nc.dram_tensor("wuqb_%d" % l, [512, 1536], BF16).ap())
                self.wukvb[l] = DramBuf("wukvb", nc.dram_tensor("wukvb_%d" % l, [512, 2048], BF16).ap())
        self.UT = DramBuf("UT", nc.dram_tensor("UT", [6144, T], F32).ap())
        self.V = DramBuf("V", nc.dram_tensor("Vs", [T, 2048], F32).ap())
        self.OF = DramBuf("OF", nc.dram_tensor("OFs", [2048, T], F32).ap())
        self.MIXT = DramBuf("MIXT", nc.dram_tensor("MIXT", [2048, T], BF16).ap())
        self.QN = DramBuf("QN", nc.dram_tensor("QN", [8, 128, T], BF16).ap())
        self.QR = DramBuf("QR", nc.dram_tensor("QR", [8, 64, T], BF16).ap())
        self.KN = DramBuf("KN", nc.dram_tensor("KN", [8, 128, T], BF16).ap())
        self.KR = DramBuf("KR", nc.dram_tensor("KR", [8, 64, T], BF16).ap())
        self.VA = DramBuf("VA", nc.dram_tensor("VA", [T, 1024], BF16).ap())

        self.w13b = {}
        self.w2b = {}
        for l in layers:
            for s in range(2):
                self.w13b[(l, s)] = DramBuf(
                    "w13b", nc.dram_tensor("w13b_%d_%d" % (l, s), [NFC, 128, KC, 256], BF16).ap()
                )
                self.w2b[(l, s)] = DramBuf(
                    "w2b", nc.dram_tensor("w2b_%d_%d" % (l, s), [FF, D], BF16).ap()
                )

        self.ident = P.sb("ident", [128, 128], F32)
        self.ones = P.sb("ones", [128, 128], F32)
        self.modc = P.sb("modc", [128, L * 144], F32)
        self.gsc = P.sb("gsc", [128, L * 3 * KC], F32)
        self.ccol = P.sb("ccol", [128, KC], F32)
        self.epsc = P.sb("epsc", [128, 1], F32)
        self.onec = P.sb("onec", [128, 1], F32)
        self.tmask = P.sb("tmask", [128, NB], F32)
        self.kbias = P.sb("kbias", [128, NB], F32)
        self.masks = P.sb("masks", [128, 6, 128], F32)
        self.ngc = P.sb("ngc", [128, L * 3 * KC], F32)
        self.abc = P.sb("abc", [128, L * 144], F32)
        self.wcvs = {}
        for l in layers:
            wb_ = Buf("wconv%d" % l)
            wb_.nobar = True
            P.bufs.append(wb_)
            self.wcvs[l] = wb_
        self.ps = []
        for i in range(8):
            t = nc.alloc_psum_tensor("psb%d" % i, [128, 512], F32)
            b = Buf("ps%d" % i, t)
            P.bufs.append(b)
            self.ps.append(b)
        self.psi = 0
        self.psa = 0
        self.psb = 0
        P.phase_mark()

        self.build()
        P.barrier()
        P.emit()

    def nps(self):
        b = self.ps[self.psi % 8]
        self.psi += 1
        return b

    def nps_a(self):
        b = self.ps[self.psa % 4]
        self.psa += 1
        return b

    def nps_b(self):
        b = self.ps[4 + self.psb % 4]
        self.psb += 1
        return b

    def build(self):
        self.phase_consts()
        self.phase_wconv()
        self.phase_mod()
        first = True
        for l in self.layers:
            self.phase_ffn(l, 0, first)
            first = False
            if not self.ffn_only:
                self.phase_mixer(l)
            self.phase_ffn(l, 1, False)

    def phase_consts(self):
        P = self.P
        P.dma("sp", self.ident[:], self.ident_d[:, :], self.ident, writes=[self.ident])
        P.dma("sp", self.ccol[:], self.c_col[:, :], self.ccol, writes=[self.ccol])
        P.dma("sp", self.ngc[:].rearrange("p (l k) -> p l k", l=DEPTH),
              self.norm_g_col.rearrange("l p k -> p l k"), self.ngc, writes=[self.ngc])
        P.dma("sp", self.abc[:].rearrange("p (l k) -> p l k", l=DEPTH),
              self.ada_b_col.rearrange("l p k -> p l k"), self.abc, writes=[self.abc])
        P.op("dve", lambda e: e.memset(self.ones[:], 1.0), writes=[self.ones])
        P.op("dve", lambda e: e.memset(self.epsc[:], EPS), writes=[self.epsc])
        P.op("dve", lambda e: e.memset(self.onec[:], 1.0), writes=[self.onec])
        P.dma("sp", self.tmask[:], self.tmask_d[:, :], self.tmask, writes=[self.tmask])
        P.dma("sp", self.kbias[:], self.kbias_d[:, :], self.kbias, writes=[self.kbias])
        P.dma("sp", self.masks[:], self.masks_d.rearrange("m p c -> p m c"), self.masks, writes=[self.masks])

    def phase_wconv(self):
        P = self.P
        for l in self.layers:
            self.wcv = self.wcvs[l]
            for s in range(2):
                w13 = self.ffn_w13[l][s]
                dst = self.w13b[(l, s)]
                for half in range(2):
                    for j in range(NFC):
                        src = w13[:, half * FF + j * 128: half * FF + (j + 1) * 128].rearrange(
                            "(kc p) c -> p kc c", p=128)
                        P.dma("pool", dst.t[j, :, :, half * 128:(half + 1) * 128], src,
                              self.wcv, dram_w=[dst])
                w2 = self.ffn_w2[l][s]
                dst2 = self.w2b[(l, s)]
                for r0 in range(0, FF, 512):
                    P.dma("pool", dst2.t[r0:r0 + 512, :], w2[r0:r0 + 512, :], self.wcv, dram_w=[dst2])
            if not self.ffn_only:
                src_in = self.ev_w_in[l // 2] if l % 2 == 0 else self.od_w_in[l // 2]
                src_out = self.ev_w_out[l // 2] if l % 2 == 0 else self.od_w_out[l // 2]
                for r0 in range(0, D, 512):
                    P.dma("pool", self.winb[l].t[r0:r0 + 512, :], src_in[r0:r0 + 512, :], self.wcv,
                          dram_w=[self.winb[l]])
                    P.dma("pool", self.woutb[l].t[r0:r0 + 512, :], src_out[r0:r0 + 512, :], self.wcv,
                          dram_w=[self.woutb[l]])
                if l % 2 == 0:
                    P.dma("pool", self.wuqb[l].t[:, :], self.mla_w_uq[l // 2], self.wcv, dram_w=[self.wuqb[l]])
                    P.dma("pool", self.wukvb[l].t[:, :], self.mla_w_ukv[l // 2], self.wcv, dram_w=[self.wukvb[l]])
            dbs = [self.w13b[(l, 0)], self.w13b[(l, 1)], self.w2b[(l, 0)], self.w2b[(l, 1)]]
            if not self.ffn_only:
                dbs += [self.winb[l], self.woutb[l]]
                if l % 2 == 0:
                    dbs += [self.wuqb[l], self.wukvb[l]]
            ev = (self.wcv.sem, 16 * self.wcv.cnt)
            for db in dbs:
                db.pending = {id(self.wcv.sem): ev}
                db.fence()

    def phase_mod(self):
        P = self.P
        CB = 512
        cond = P.sb("cond", [128, KC], F32)
        P.op("act", lambda e: e.activation(out=cond[:], in_=self.ccol[:], func=AF.Silu),
             reads=[self.ccol], writes=[cond])
        slots = [P.sb("adaw%d" % i, [128, KC, CB], F32) for i in range(2)]
        n = 0
        for l in self.layers:
            pst = self.nps()
            for cb in range(9 * D // CB):
                sl = slots[n % 2]
                n += 1
                src = self.ada_w[l][:, cb * CB:(cb + 1) * CB].rearrange("(kc p) c -> p kc c", p=128)
                P.dma("sp", sl[:], src, sl, writes=[sl])
                for jj in range(CB // 128):
                    j = cb * (CB // 128) + jj
                    for kc in range(KC):
                        P.op("pe", lambda e, sl=sl, kc=kc, jj=jj, j=j, pst=pst: e.matmul(
                            pst[:, j:j + 1], lhsT=sl[:, kc, jj * 128:(jj + 1) * 128],
                            rhs=cond[:, kc:kc + 1], start=(kc == 0), stop=(kc == KC - 1)),
                            reads=[sl, cond], writes=[pst])
            P.op("dve", lambda e, l=l, pst=pst: e.tensor_tensor(
                out=self.modc[:, l * 144:(l + 1) * 144], in0=pst[:, 0:144],
                in1=self.abc[:, l * 144:(l + 1) * 144], op=ALU.add),
                reads=[pst, self.abc], writes=[self.modc])
            for s in range(3):
                sc = self.modc[:, l * 144 + (s * 3 + 1) * KC: l * 144 + (s * 3 + 2) * KC]
                o = (l * 3 + s) * KC
                P.op("dve", lambda e, sc=sc, o=o: e.scalar_tensor_tensor(
                    out=self.gsc[:, o:o + KC], in0=sc, scalar=1.0, in1=self.ngc[:, o:o + KC],
                    op0=ALU.add, op1=ALU.mult),
                    reads=[self.modc, self.ngc], writes=[self.gsc])
        P.phase_reset()

    def mod_col(self, l, s, j):
        o = l * 144 + (s * 3 + j) * KC
        return self.modc[:, o:o + KC]

    def make_gate_b(self, l, s, mult):
        P = self.P
        gb = P.sb("gate_b", [128, D], F32)
        gtmp = [P.sb("gtmp%d" % i, [128, 128], F32) for i in range(2)]
        gc = self.mod_col(l, s, 2)
        for q in range(4):
            pst = self.nps()
            for kk in range(4):
                kc = q * 4 + kk
                g = gtmp[kc % 2]
                P.op("dve", lambda e, g=g, kc=kc: e.tensor_scalar(
                    out=g[:], in0=self.ident[:], scalar1=gc[:, kc:kc + 1], scalar2=None, op0=ALU.mult),
                    reads=[self.ident, self.modc], writes=[g])
                P.op("pe", lambda e, g=g, kk=kk, pst=pst: e.matmul(
                    pst[:, kk * 128:(kk + 1) * 128], lhsT=self.ones[:], rhs=g[:], start=True, stop=True),
                    reads=[self.ones, g], writes=[pst])
            P.op("act", lambda e, q=q, pst=pst: e.activation(
                out=gb[:, q * 512:(q + 1) * 512], in_=pst[:], func=AF.Copy, scale=float(mult)),
                reads=[pst], writes=[gb])
        return gb

    def norm_transpose(self, xt, hT, col0, gs, sh, xn_slots, cnt, stats):
        P = self.P
        ss, rstd = stats[cnt % len(stats)]
        xn = xn_slots[cnt % len(xn_slots)]
        P.op("act", lambda e: e.activation(out=xn[:], in_=xt[:], func=AF.Square, accum_out=ss[:]),
             reads=[xt], writes=[xn, ss])
        P.op("act", lambda e: e.activation(out=ss[:], in_=ss[:], func=AF.Sqrt, bias=self.epsc[:, 0:1],
                                            scale=1.0 / D), reads=[ss, self.epsc], writes=[ss])
        P.op("dve", lambda e: e.reciprocal(out=rstd[:], in_=ss[:]), reads=[ss], writes=[rstd])
        P.op("pool", lambda e: e.tensor_scalar(out=xn[:], in0=xt[:], scalar1=rstd[:, 0:1], scalar2=None,
                                                op0=ALU.mult), reads=[xt, rstd], writes=[xn])
        for q in range(4):
            pst = self.nps()
            for kk in range(4):
                kc = q * 4 + kk
                P.op("pe", lambda e, kc=kc, kk=kk, pst=pst: e.transpose(
                    pst[:, kk * 128:(kk + 1) * 128], xn[:, kc * 128:(kc + 1) * 128], self.ident[:]),
                    reads=[xn, self.ident], writes=[pst])
            for kk in range(4):
                kc = q * 4 + kk
                if q % 2 == 0:
                    P.op("act", lambda e, kc=kc, kk=kk, pst=pst: e.activation(
                        out=hT[:, kc, col0:col0 + 128], in_=pst[:, kk * 128:(kk + 1) * 128],
                        func=AF.Identity, bias=sh[:, kc:kc + 1], scale=gs[:, kc:kc + 1]),
                        reads=[pst, self.modc, self.gsc], writes=[hT])
                else:
                    P.op("dve", lambda e, kc=kc, kk=kk, pst=pst: e.tensor_scalar(
                        out=hT[:, kc, col0:col0 + 128], in0=pst[:, kk * 128:(kk + 1) * 128],
                        scalar1=gs[:, kc:kc + 1], scalar2=sh[:, kc:kc + 1], op0=ALU.mult, op1=ALU.add),
                        reads=[pst, self.modc, self.gsc], writes=[hT])

    def phase_ffn(self, l, s, first):
        P = self.P
        T, TT = self.T, self.TT
        TS = TT // 128
        sub = 0 if s == 0 else 2
        w13b = self.w13b[(l, s)]
        w2b = self.w2b[(l, s)]
        xin = self.x if first else self.y.t
        self.y.fence()
        gs = self.gsc[:, (l * 3 + sub) * KC:(l * 3 + sub + 1) * KC]
        sh = self.mod_col(l, sub, 0)
        gb = self.make_gate_b(l, sub, 0.5)
        hT = P.sb("hT", [128, KC, TT], BF16)
        actT = P.sb("actT", [128, NFC, TT], BF16)
        xts = [P.sb("xt%d" % i, [128, D], F32) for i in range(TS)]
        xns = [P.sb("xn%d" % i, [128, D], F32) for i in range(2)]
        w13s = [P.sb("w13s%d" % i, [128, KC, 256], BF16) for i in range(5)]
        w2s = [P.sb("w2s%d" % i, [128, 4, 512], BF16) for i in range(6)]
        sgs = [P.sb("sg%d" % i, [128, 512], F32) for i in range(2)]
        yts = [P.sb("yt%d" % i, [128, 1024], F32) for i in range(2)]
        stats = [(P.sb("ss%d" % i, [128, 1], F32), P.sb("rstd%d" % i, [128, 1], F32)) for i in range(2)]
        n13 = 0
        n2 = 0
        nsg = 0
        nyt = 0
        cnt = 0
        for t0 in range(0, T, TT):
            for si in range(TS):
                xt = xts[si]
                P.dma("sp", xt[:], xin[t0 + si * 128:t0 + (si + 1) * 128, :], xt, writes=[xt],
                      dram_r=[self.y])
                self.norm_transpose(xt, hT, si * 128, gs, sh, xns, cnt, stats)
                cnt += 1
            for j in range(NFC):
                wsl = w13s[n13 % 5]
                n13 += 1
                P.dma("sp", wsl[:], w13b.t[j], wsl, writes=[wsl], dram_r=[w13b])
                for c0 in range(0, TT, 512):
                    pg = self.nps()
                    pu = self.nps()
                    for kc in range(KC):
                        P.op("pe", lambda e, wsl=wsl, kc=kc, pg=pg, c0=c0: e.matmul(
                            pg[:], lhsT=wsl[:, kc, 0:128], rhs=hT[:, kc, c0:c0 + 512],
                            start=(kc == 0), stop=(kc == KC - 1)), reads=[wsl, hT], writes=[pg])
                    for kc in range(KC):
                        P.op("pe", lambda e, wsl=wsl, kc=kc, pu=pu, c0=c0: e.matmul(
                            pu[:], lhsT=wsl[:, kc, 128:256], rhs=hT[:, kc, c0:c0 + 512],
                            start=(kc == 0), stop=(kc == KC - 1)), reads=[wsl, hT], writes=[pu])
                    sg = sgs[nsg % 2]
                    nsg += 1
                    P.op("act", lambda e, sg=sg, pg=pg: e.activation(out=sg[:], in_=pg[:], func=AF.Silu),
                         reads=[pg], writes=[sg])
                    P.op("dve", lambda e, sg=sg, pu=pu, j=j, c0=c0: e.tensor_tensor(
                        out=actT[:, j, c0:c0 + 512], in0=sg[:], in1=pu[:], op=ALU.mult),
                        reads=[sg, pu], writes=[actT])
            for dh in range(2):
                banks = [[self.nps(), self.nps()] for _ in range(TS)] if TS <= 4 else None
                assert banks is not None
                for jg in range(0, NFC, 4):
                    wsl = w2s[n2 % 6]
                    wsl2 = w2s[(n2 + 1) % 6]
                    n2 += 2
                    for dq, w in ((0, wsl), (1, wsl2)):
                        c0 = dh * 1024 + dq * 512
                        P.dma("sp", w[:], w2b.t[jg * 128:(jg + 4) * 128, c0:c0 + 512].rearrange(
                            "(j p) c -> p j c", p=128), w, writes=[w], dram_r=[w2b])
                    for jj in range(4):
                        j = jg + jj
                        for si in range(TS):
                            for dq, w in ((0, wsl), (1, wsl2)):
                                pb = banks[si][dq]
                                P.op("pe", lambda e, w=w, jj=jj, j=j, si=si, pb=pb: e.matmul(
                                    pb[:], lhsT=actT[:, j, si * 128:(si + 1) * 128], rhs=w[:, jj, :],
                                    start=(j == 0), stop=(j == NFC - 1)), reads=[w, actT], writes=[pb])
                for si in range(TS):
                    yt = yts[nyt % 2]
                    nyt += 1
                    for dq in range(2):
                        c0 = dh * 1024 + dq * 512
                        pb = banks[si][dq]
                        P.op("dve", lambda e, pb=pb, yt=yt, dq=dq, c0=c0, si=si: e.tensor_tensor(
                            out=yt[:, dq * 512:(dq + 1) * 512], in0=pb[:], in1=gb[:, c0:c0 + 512],
                            op=ALU.mult), reads=[pb, gb], writes=[yt])
                        P.op("pool", lambda e, yt=yt, dq=dq, c0=c0, si=si: e.tensor_tensor(
                            out=yt[:, dq * 512:(dq + 1) * 512], in0=yt[:, dq * 512:(dq + 1) * 512],
                            in1=xts[si][:, c0:c0 + 512], op=ALU.add), reads=[yt, xts[si]], writes=[yt])
                    P.dma("pool", self.y.t[t0 + si * 128:t0 + (si + 1) * 128, dh * 1024:(dh + 1) * 1024],
                          yt[:], yt, reads=[yt], dram_w=[self.y])
        P.phase_reset()

    def phase_mixer(self, l):
        import os
        skip = os.environ.get("K_SKIP", "")
        if l % 2 == 0 and skip:
            if "inproj" not in skip:
                self.phase_inproj_ev(l)
            if "scan" not in skip:
                self.phase_scan(l, dict(H=8, ndk=1, ndv=1, C=32, q=0, kf=1024, kb=2048, sg=5120, lff=3072,
                                        lfb=4096, ong=self.hg_ong_col[l // 2], mix0=0, hgroup=4))
            if "attn" not in skip:
                self.phase_attn(l)
            if "outproj" not in skip:
                self.phase_outproj(l)
            return
        if l % 2 == 1:
            self.phase_inproj_gla(l)
            self.phase_scan(l, dict(H=4, ndk=2, ndv=4, C=128, q=0, kf=1024, kb=1024, sg=2048, lff=4096,
                                    lfb=5120, ong=self.gla_ong_col[l // 2], mix0=0, hgroup=2))
        else:
            self.phase_inproj_ev(l)
            self.phase_scan(l, dict(H=8, ndk=1, ndv=1, C=32, q=0, kf=1024, kb=2048, sg=5120, lff=3072,
                                    lfb=4096, ong=self.hg_ong_col[l // 2], mix0=0, hgroup=4))
            self.phase_attn(l)
        self.phase_outproj(l)

    def load_norm_tile(self, l, t0, hT, xts, xns, stats, cnt0):
        P = self.P
        gs = self.gsc[:, (l * 3 + 1) * KC:(l * 3 + 2) * KC]
        sh = self.mod_col(l, 1, 0)
        for si in range(4):
            xt = xts[si % len(xts)]
            P.dma("sp", xt[:], self.y.t[t0 + si * 128:t0 + (si + 1) * 128, :], xt, writes=[xt],
                  dram_r=[self.y])
            self.norm_transpose(xt, hT, si * 128, gs, sh, xns, cnt0 + si, stats)

    def fm_mm(self, pst, wblk, c0, nc_, hT, prow=None):
        P = self.P
        for kc in range(KC):
            P.op("pe", lambda e, kc=kc: e.matmul(
                pst[0:nc_, :], lhsT=wblk[:, kc, c0:c0 + nc_], rhs=hT[:, kc, :],
                start=(kc == 0), stop=(kc == KC - 1)), reads=[wblk, hT], writes=[pst])

    def tm_mm(self, pst, wblk, si, hT, ncols=512):
        P = self.P
        for kc in range(KC):
            P.op("pe", lambda e, kc=kc: e.matmul(
                pst[:, 0:ncols], lhsT=hT[:, kc, si * 128:(si + 1) * 128], rhs=wblk[:, kc, 0:ncols],
                start=(kc == 0), stop=(kc == KC - 1)), reads=[wblk, hT], writes=[pst])

    def store_fm(self, st, row0, t0, nrows=128):
        self.P.dma("pool", self.UT.t[row0:row0 + nrows, t0:t0 + 512], st[0:nrows, :], st, reads=[st],
                   dram_w=[self.UT])

    def phase_inproj_gla(self, l):
        P = self.P
        T = self.T
        e_ = l // 2
        winb = self.winb[l]
        self.y.fence()
        self.UT.fence(); self.V.fence()
        hT = P.sb("hT", [128, KC, 512], BF16)
        xts = [P.sb("xt%d" % i, [128, D], F32) for i in range(2)]
        xns = [P.sb("xn%d" % i, [128, D], F32) for i in range(2)]
        stats = [(P.sb("ss%d" % i, [128, 1], F32), P.sb("rstd%d" % i, [128, 1], F32)) for i in range(2)]
        wbl = [P.sb("wbl%d" % i, [128, KC, 512], BF16) for i in range(3)]
        sts = [P.sb("st%d" % i, [128, 512], F32) for i in range(4)]
        rT = [P.sb("rT%d" % i, [16, 512], F32) for i in range(2)]
        w2s = P.sb("gkw2", [16, 2, 1024], F32)
        bcol = P.sb("gkb", [128, 16], F32)
        nbcol = P.sb("ngkb", [128, 16], F32)
        e1 = [P.sb("e1_%d" % i, [128, 512], F32) for i in range(2)]
        mrow = P.sb("mrow", [128, 512], F32)
        P.dma("sp", w2s[:], self.gla_w2[e_].rearrange("d r c -> r d c"), w2s, writes=[w2s])
        P.dma("sp", bcol[:].rearrange("p (d k) -> p d k", d=2), self.gla_b_col[e_].rearrange("d p k -> p d k"),
              bcol, writes=[bcol])
        P.op("dve", lambda e: e.tensor_scalar(out=nbcol[:], in0=bcol[:], scalar1=-1.0, scalar2=None,
                                               op0=ALU.mult), reads=[bcol], writes=[nbcol])
        nw = 0
        nst = 0
        cnt = 0
        for t0 in range(0, T, 512):
            self.load_norm_tile(l, t0, hT, xts, xns, stats, cnt)
            cnt += 4
            P.dma("sp", mrow[:], self.tmask_row_d[0:1, t0:t0 + 512].partition_broadcast(128), mrow, writes=[mrow])
            for b in range(13):
                wb = wbl[nw % 3]
                nw += 1
                ncol = 512 if b < 12 else 32
                P.dma("sp", wb[:, :, 0:ncol], winb.t[:, b * 512:b * 512 + ncol].rearrange(
                    "(kc p) c -> p kc c", p=128), wb, writes=[wb], dram_r=[winb])
                if b < 4 or (8 <= b < 12):
                    for cc in range(4):
                        pst = self.nps()
                        self.fm_mm(pst, wb, cc * 128, 128, hT)
                        st = sts[nst % 4]
                        nst += 1
                        if b < 2:
                            P.op("act", lambda e, st=st, pst=pst: e.activation(
                                out=st[:], in_=pst[:], func=AF.Copy, scale=1.0 / 16.0), reads=[pst], writes=[st])
                            row0 = b * 512 + cc * 128
                        elif b < 4:
                            P.op("dve", lambda e, st=st, pst=pst: e.tensor_copy(out=st[:], in_=pst[:]),
                                 reads=[pst], writes=[st])
                            row0 = b * 512 + cc * 128
                        else:
                            P.op("act", lambda e, st=st, pst=pst: e.activation(
                                out=st[:], in_=pst[:], func=AF.Silu), reads=[pst], writes=[st])
                            row0 = 2048 + (b - 8) * 512 + cc * 128
                        self.store_fm(st, row0, t0)
                elif b < 8:
                    for si in range(4):
                        pst = self.nps()
                        self.tm_mm(pst, wb, si, hT)
                        st = sts[nst % 4]
                        nst += 1
                        ti = (t0 // 128) + si
                        P.op("act", lambda e, st=st, pst=pst, ti=ti: e.activation(
                            out=st[:], in_=pst[:], func=AF.Copy, scale=self.tmask[:, ti:ti + 1]),
                            reads=[pst, self.tmask], writes=[st])
                        P.dma("pool", self.V.t[t0 + si * 128:t0 + (si + 1) * 128, (b - 4) * 512:(b - 3) * 512],
                              st[:], st, reads=[st], dram_w=[self.V])
                else:
                    for d_ in range(2):
                        pst = self.nps()
                        self.fm_mm(pst, wb, d_ * 16, 16, hT)
                        P.op("dve", lambda e, d_=d_, pst=pst: e.tensor_copy(out=rT[d_][:], in_=pst[0:16, :]),
                             reads=[pst], writes=[rT[d_]])
                    for d_ in range(2):
                        for cc in range(8):
                            pst = self.nps()
                            P.op("pe", lambda e, d_=d_, cc=cc, pst=pst: e.matmul(
                                pst[:], lhsT=w2s[:, d_, cc * 128:(cc + 1) * 128], rhs=rT[d_][:],
                                start=True, stop=True), reads=[w2s, rT[d_]], writes=[pst])
                            ee = e1[cc % 2]
                            P.op("act", lambda e, d_=d_, cc=cc, pst=pst, ee=ee: e.activation(
                                out=ee[:], in_=pst[:], func=AF.Exp, scale=-1.0,
                                bias=nbcol[:, d_ * 8 + cc:d_ * 8 + cc + 1]), reads=[pst, nbcol], writes=[ee])
                            P.op("act", lambda e, ee=ee: e.activation(
                                out=ee[:], in_=ee[:], func=AF.Ln, bias=self.onec[:, 0:1]),
                                reads=[ee, self.onec], writes=[ee])
                            st = sts[nst % 4]
                            nst += 1
                            P.op("dve", lambda e, st=st, ee=ee: e.scalar_tensor_tensor(
                                out=st[:], in0=ee[:], scalar=-1.0 / 16.0, in1=mrow[:], op0=ALU.mult, op1=ALU.mult),
                                reads=[ee, mrow], writes=[st])
                            self.store_fm(st, 4096 + d_ * 1024 + cc * 128, t0)
        P.phase_reset()

    def phase_scan(self, l, cfg):
        P = self.P
        T = self.T
        NBLK = T // 128
        H, ndk, ndv, C = cfg["H"], cfg["ndk"], cfg["ndv"], cfg["C"]
        nsub = 128 // C
        dv = ndv * 128
        mi = 0 if C == 128 else 3
        maskf = self.masks[:, mi + 0, :]
        maskb = self.masks[:, mi + 1, :]
        rm = self.masks[:, mi + 2, :]
        self.UT.fence(); self.V.fence(); self.OF.fence(); self.MIXT.fence()
        ong = P.sb("ong", [128, ndv], F32)
        P.dma("sp", ong[:], cfg["ong"], ong, writes=[ong])
        G = cfg["hgroup"]
        NS = 2

        def mk(name, shape, dt=F32):
            return [[P.sb("%s_%d_%d" % (name, g, i), shape, dt) for i in range(NS)] for g in range(G)]

        qs, ks, lfs = mk("q", [128, ndk, 128]), mk("k", [128, ndk, 128]), mk("lf", [128, ndk, 128])
        cums, args = mk("cum", [128, ndk, 128]), mk("arg", [128, ndk, 128])
        Eqs, Eks = mk("Eq", [128, ndk, 128]), mk("Ek", [128, ndk, 128])
        v128 = mk("v128", [128, dv])
        vsub = mk("vsub", [C, nsub, dv]) if nsub > 1 else None
        els = mk("el", [128, ndk, nsub])
        ktm = mk("ktm", [128, ndk * 128])
        att = mk("att", [128, 128])
        osb = mk("osb", [128, dv])
        sqb = mk("sqb", [128, dv])
        sgb = mk("sgb", [128, ndv, 128])
        ofb = mk("ofb", [128, ndv, 128])
        rsb = mk("rsb", [128, 128])
        mxb = mk("mxb", [128, ndv, 128], BF16)
        S = [[P.sb("S_%d_%d" % (g, kk), [128, dv], F32) for kk in range(ndk)] for g in range(G)]
        S1 = [[P.sb("S1_%d_%d" % (g, kk), [128, dv], F32) for kk in range(ndk)] for g in range(G)]
        cnts = [0] * G

        def block(g, h, dirn, bi):
            n = cnts[g] % NS
            cnts[g] += 1
            t0 = bi * 128
            q, k, lf, cum, arg, Eq, Ek = qs[g][n], ks[g][n], lfs[g][n], cums[g][n], args[g][n], Eqs[g][n], Eks[g][n]
            V1 = v128[g][n]
            el = els[g][n]
            krow = cfg["kf"] if dirn == 0 else cfg["kb"]
            lrow = cfg["lff"] if dirn == 0 else cfg["lfb"]

            def ld(dst, row):
                P.dma("sp", dst[:], self.UT.t[row + h * ndk * 128: row + (h + 1) * ndk * 128, t0:t0 + 128].rearrange(
                    "(k p) t -> p k t", p=128), dst, writes=[dst], dram_r=[self.UT])
            ld(q, cfg["q"]); ld(k, krow); ld(lf, lrow)
            P.dma("sp", V1[:], self.V.t[t0:t0 + 128, h * dv:(h + 1) * dv], V1, writes=[V1], dram_r=[self.V])
            if nsub > 1:
                Vs = vsub[g][n]
                P.dma("sp", Vs[:], self.V.t[t0:t0 + 128, h * dv:(h + 1) * dv].rearrange("(c p) d -> p c d", p=C),
                      Vs, writes=[Vs], dram_r=[self.V])
            if dirn == 1:
                sg_, of_ = sgb[g][n], ofb[g][n]
                P.dma("sp", sg_[:], self.UT.t[cfg["sg"] + h * dv: cfg["sg"] + (h + 1) * dv, t0:t0 + 128].rearrange(
                    "(k p) t -> p k t", p=128), sg_, writes=[sg_], dram_r=[self.UT])
                P.dma("sp", of_[:], self.OF.t[h * dv:(h + 1) * dv, t0:t0 + 128].rearrange("(k p) t -> p k t", p=128),
                      of_, writes=[of_], dram_r=[self.OF])
            for kk in range(ndk):
                P.op("dve", lambda e, kk=kk: e.tensor_tensor_scan(
                    out=cum[:, kk, :], data0=rm, data1=lf[:, kk, :], initial=0.0, op0=ALU.mult, op1=ALU.add),
                    reads=[lf, self.masks], writes=[cum])
                if dirn == 0:
                    asrc = cum
                else:
                    P.op("pool", lambda e, kk=kk: e.tensor_tensor(out=arg[:, kk, :], in0=cum[:, kk, :],
                                                                   in1=lf[:, kk, :], op=ALU.subtract),
                         reads=[cum, lf], writes=[arg])
                    for c in range(nsub):
                        P.op("dve", lambda e, kk=kk, c=c: e.tensor_scalar(
                            out=arg[:, kk, c * C:(c + 1) * C], in0=arg[:, kk, c * C:(c + 1) * C], scalar1=-1.0,
                            scalar2=cum[:, kk, (c + 1) * C - 1:(c + 1) * C], op0=ALU.mult, op1=ALU.add),
                            reads=[arg, cum], writes=[arg])
                    asrc = arg
                P.op("act", lambda e, kk=kk, asrc=asrc: e.activation(out=Eq[:, kk, :], in_=asrc[:, kk, :], func=AF.Exp),
                     reads=[asrc], writes=[Eq])
                P.op("act", lambda e, kk=kk, asrc=asrc: e.activation(out=Ek[:, kk, :], in_=asrc[:, kk, :], func=AF.Exp,
                                                                      scale=-1.0), reads=[asrc], writes=[Ek])
                for c in range(nsub):
                    P.op("act", lambda e, kk=kk, c=c: e.activation(
                        out=el[:, kk, c:c + 1], in_=cum[:, kk, (c + 1) * C - 1:(c + 1) * C], func=AF.Exp),
                        reads=[cum], writes=[el])
                P.op("pool", lambda e, kk=kk: e.tensor_tensor(out=Eq[:, kk, :], in0=Eq[:, kk, :], in1=q[:, kk, :],
                                                               op=ALU.mult), reads=[Eq, q], writes=[Eq])
                P.op("dve", lambda e, kk=kk: e.tensor_tensor(out=Ek[:, kk, :], in0=Ek[:, kk, :], in1=k[:, kk, :],
                                                              op=ALU.mult), reads=[Ek, k], writes=[Ek])
            Qp, Kp = Eq, Ek
            pa = self.nps_a()
            for kk in range(ndk):
                P.op("pe", lambda e, kk=kk: e.matmul(pa[:, 0:128], lhsT=Kp[:, kk, :], rhs=Qp[:, kk, :],
                                                     start=(kk == 0), stop=(kk == ndk - 1)),
                     reads=[Kp, Qp], writes=[pa])
            at = att[g][n]
            mk_ = maskf if dirn == 0 else maskb
            P.op("dve", lambda e: e.tensor_tensor(out=at[:], in0=pa[:, 0:128], in1=mk_, op=ALU.mult),
                 reads=[pa, self.masks], writes=[at])
            po = self.nps_a()
            subs = list(range(nsub)) if dirn == 0 else list(range(nsub - 1, -1, -1))
            for c in subs:
                cs = slice(c * C, (c + 1) * C)
                for dvc in range(ndv):
                    ocol = slice(dvc * 128 + c * C, dvc * 128 + (c + 1) * C)
                    P.op("pe", lambda e, dvc=dvc, ocol=ocol, cs=cs: e.matmul(
                        po[:, ocol], lhsT=V1[:, dvc * 128:(dvc + 1) * 128], rhs=at[:, cs], start=True, stop=False),
                        reads=[V1, at], writes=[po])
                    for kk in range(ndk):
                        P.op("pe", lambda e, dvc=dvc, ocol=ocol, cs=cs, kk=kk: e.matmul(
                            po[:, ocol], lhsT=S[g][kk][:, dvc * 128:(dvc + 1) * 128], rhs=Qp[:, kk, cs],
                            start=False, stop=(kk == ndk - 1)), reads=[S[g][kk], Qp], writes=[po])
                pt = self.nps_b()
                for kk in range(ndk):
                    P.op("pe", lambda e, kk=kk, cs=cs, pt=pt: e.transpose(pt[0:C, kk * 128:(kk + 1) * 128],
                                                                           Kp[:, kk, cs], self.ident[:]),
                         reads=[Kp, self.ident], writes=[pt])
                kt = ktm[g][n]
                P.op("act", lambda e, pt=pt: e.activation(out=kt[0:C, :], in_=pt[0:C, 0:ndk * 128], func=AF.Copy),
                     reads=[pt], writes=[kt])
                for kk in range(ndk):
                    psS = self.nps_b()
                    if nsub > 1:
                        rhs_ = vsub[g][n][:, c, :]
                        rb = vsub[g][n]
                    else:
                        rhs_ = V1[:, :]
                        rb = V1
                    P.op("pe", lambda e, kk=kk, psS=psS, rhs_=rhs_: e.matmul(
                        psS[:, 0:dv], lhsT=kt[0:C, kk * 128:(kk + 1) * 128], rhs=rhs_, start=True, stop=True),
                        reads=[kt, rb], writes=[psS])
                    P.op("act", lambda e, kk=kk, c=c: e.activation(
                        out=S1[g][kk][:], in_=S[g][kk][:], func=AF.Copy, scale=el[:, kk, c:c + 1]),
                        reads=[S[g][kk], el], writes=[S1[g][kk]])
                    P.op("dve", lambda e, kk=kk, c=c, psS=psS: e.scalar_tensor_tensor(
                        out=S[g][kk][:], in0=psS[:, 0:dv], scalar=el[:, kk, c:c + 1], in1=S1[g][kk][:],
                        op0=ALU.mult, op1=ALU.add), reads=[psS, el, S1[g][kk]], writes=[S[g][kk]])
            ob = osb[g][n]
            if dirn == 0:
                P.op("act", lambda e: e.activation(out=ob[:], in_=po[:, 0:dv], func=AF.Copy), reads=[po], writes=[ob])
                P.dma("pool", self.OF.t[h * dv:(h + 1) * dv, t0:t0 + 128].rearrange("(k p) t -> p k t", p=128),
                      ob[:].rearrange("p (k t) -> p k t", k=ndv), ob, reads=[ob], dram_w=[self.OF])
            else:
                sg_, of_ = sgb[g][n], ofb[g][n]
                sq = sqb[g][n]
                rs = rsb[g][n]
                mx = mxb[g][n]
                P.op("dve", lambda e: e.tensor_tensor(out=ob[:], in0=po[:, 0:dv],
                                                      in1=of_[:].rearrange("p k t -> p (k t)"), op=ALU.add),
                     reads=[po, of_], writes=[ob])
                P.op("act", lambda e: e.activation(out=sq[:], in_=ob[:], func=AF.Square), reads=[ob], writes=[sq])
                pn = self.nps_a()
                for dvc in range(ndv):
                    P.op("pe", lambda e, dvc=dvc: e.matmul(pn[:, 0:128], lhsT=self.ones[:],
                                                           rhs=sq[:, dvc * 128:(dvc + 1) * 128],
                                                           start=(dvc == 0), stop=(dvc == ndv - 1)),
                         reads=[self.ones, sq], writes=[pn])
                P.op("act", lambda e: e.activation(out=rs[:], in_=pn[:, 0:128], func=AF.Sqrt,
                                                   bias=self.epsc[:, 0:1], scale=1.0 / dv),
                     reads=[pn, self.epsc], writes=[rs])
                P.op("dve", lambda e: e.reciprocal(out=rs[:], in_=rs[:]), reads=[rs], writes=[rs])
                for dvc in range(ndv):
                    P.op("pool", lambda e, dvc=dvc: e.tensor_tensor(
                        out=ob[:, dvc * 128:(dvc + 1) * 128], in0=ob[:, dvc * 128:(dvc + 1) * 128], in1=rs[:],
                        op=ALU.mult), reads=[ob, rs], writes=[ob])
                    P.op("dve", lambda e, dvc=dvc: e.scalar_tensor_tensor(
                        out=mx[:, dvc, :], in0=ob[:, dvc * 128:(dvc + 1) * 128], scalar=ong[:, dvc:dvc + 1],
                        in1=sg_[:, dvc, :], op0=ALU.mult, op1=ALU.mult), reads=[ob, ong, sg_], writes=[mx])
                r0 = cfg["mix0"] + h * dv
                P.dma("pool", self.MIXT.t[r0:r0 + dv, t0:t0 + 128].rearrange("(k p) t -> p k t", p=128),
                      mx[:], mx, reads=[mx], dram_w=[self.MIXT])

        for h0 in range(0, H, G):
            for dirn in range(2):
                for g in range(G):
                    for kk in range(ndk):
                        P.op("pool", lambda e, g=g, kk=kk: e.memset(S[g][kk][:], 0.0), writes=[S[g][kk]])
                if dirn == 1:
                    self.OF.fence()
                order = range(NBLK) if dirn == 0 else range(NBLK - 1, -1, -1)
                for bi in order:
                    for g in range(G):
                        block(g, h0 + g, dirn, bi)
        P.phase_reset()

    def phase_outproj(self, l):
        P = self.P
        T = self.T
        self.y.fence(); self.MIXT.fence()
        gb = self.make_gate_b(l, 1, 1.0)
        wo = P.sb("wo", [128, KC, D], BF16)
        P.dma("sp", wo[:], self.woutb[l].t.rearrange("(kc p) c -> p kc c", p=128), wo, writes=[wo],
              dram_r=[self.woutb[l]])
        mts = [P.sb("mt%d" % i, [128, KC, 512], BF16) for i in range(2)]
        xts = [P.sb("xt%d" % i, [128, D], F32) for i in range(2)]
        yts = [P.sb("yt%d" % i, [128, D], F32) for i in range(2)]
        n = 0
        for ti, t0 in enumerate(range(0, T, 512)):
            mt = mts[ti % 2]
            P.dma("sp", mt[:], self.MIXT.t[:, t0:t0 + 512].rearrange("(kc p) t -> p kc t", p=128), mt,
                  writes=[mt], dram_r=[self.MIXT])
            for si in range(4):
                xt = xts[n % 2]
                yt = yts[n % 2]
                n += 1
                r0 = t0 + si * 128
                P.dma("sp", xt[:], self.y.t[r0:r0 + 128, :], xt, writes=[xt], dram_r=[self.y])
                for nn in range(4):
                    pst = self.nps()
                    for kc in range(KC):
                        P.op("pe", lambda e, kc=kc, nn=nn, pst=pst, mt=mt, si=si: e.matmul(
                            pst[:], lhsT=mt[:, kc, si * 128:(si + 1) * 128], rhs=wo[:, kc, nn * 512:(nn + 1) * 512],
                            start=(kc == 0), stop=(kc == KC - 1)), reads=[mt, wo], writes=[pst])
                    cs = slice(nn * 512, (nn + 1) * 512)
                    P.op("dve", lambda e, pst=pst, yt=yt, cs=cs: e.tensor_tensor(
                        out=yt[:, cs], in0=pst[:], in1=gb[:, cs], op=ALU.mult), reads=[pst, gb], writes=[yt])
                    P.op("pool", lambda e, yt=yt, xt=xt, cs=cs: e.tensor_tensor(
                        out=yt[:, cs], in0=yt[:, cs], in1=xt[:, cs], op=ALU.add), reads=[yt, xt], writes=[yt])
                P.dma("pool", self.y.t[r0:r0 + 128, :], yt[:], yt, reads=[yt], dram_w=[self.y])
        P.phase_reset()

    def phase_inproj_ev(self, l):
        P = self.P
        T = self.T
        e_ = l // 2
        winb = self.winb[l]
        self.y.fence()
        for db in (self.UT, self.V, self.QN, self.QR, self.KN, self.KR, self.VA):
            db.fence()
        wuq = P.sb("wuq", [128, 4, 1536], BF16)
        wukv = P.sb("wukv", [128, 4, 2048], BF16)
        pswap = P.sb("pswap", [64, 64], F32)
        hT = P.sb("hT", [128, KC, 512], BF16)
        xts = [P.sb("xt%d" % i, [128, D], F32) for i in range(2)]
        xns = [P.sb("xn%d" % i, [128, D], F32) for i in range(2)]
        stats = [(P.sb("ss%d" % i, [128, 1], F32), P.sb("rstd%d" % i, [128, 1], F32)) for i in range(2)]
        wbl = [P.sb("wbl%d" % i, [128, KC, 512], BF16) for i in range(2)]
        sts = [P.sb("st%d" % i, [128, 512], F32) for i in range(5)]
        sbs = [P.sb("sb%d" % i, [128, 512], BF16) for i in range(3)]
        tmp = [P.sb("tmp%d" % i, [128, 512], F32) for i in range(3)]
        oml = P.sb("oml", [128, 16], F32)
        mrow = P.sb("mrow", [128, 512], F32)
        cq = P.sb("cq", [128, 4, 512], F32)
        ckv = P.sb("ckv", [128, 4, 512], F32)
        sq4 = P.sb("sq4", [128, 4, 512], F32)
        cqn = P.sb("cqn", [128, 4, 512], BF16)
        ckvn = P.sb("ckvn", [128, 4, 512], BF16)
        kpe = P.sb("kpe", [64, 512], F32)
        sqk = P.sb("sqk", [128, 512], F32)
        sqrq = P.sb("sqrq", [128, 512], F32)
        skp = P.sb("skp", [128, 512], F32)
        P.op("pool", lambda e: e.memset(sqk[:], 0.0), writes=[sqk])
        P.op("pool", lambda e: e.memset(sqrq[:], 0.0), writes=[sqrq])
        krot = P.sb("krot", [64, 512], F32)
        rs = P.sb("rs", [128, 512], F32)
        rope = P.sb("rope", [64, 2, 512], F32)
        gq = P.sb("gq", [128, 12], F32)
        P.dma("sp", wuq[:], self.wuqb[l].t.rearrange("(c p) n -> p c n", p=128), wuq, writes=[wuq],
              dram_r=[self.wuqb[l]])
        P.dma("sp", wukv[:], self.wukvb[l].t.rearrange("(c p) n -> p c n", p=128), wukv, writes=[wukv],
              dram_r=[self.wukvb[l]])
        P.dma("sp", pswap[:], self.pswap_d[:, :], pswap, writes=[pswap])
        P.dma("sp", gq[:, 0:4], self.mla_qa_col[e_], gq, writes=[gq])
        P.dma("sp", gq[:, 4:8], self.mla_kva_col[e_], gq, writes=[gq])
        P.dma("sp", gq[:, 8:10], self.mla_qn_col[e_], gq, writes=[gq])
        P.dma("sp", gq[:, 10:12], self.mla_kn_col[e_], gq, writes=[gq])
        P.op("dve", lambda e: e.tensor_scalar(out=gq[:, 8:10], in0=gq[:, 8:10], scalar1=float(192 ** -0.5),
                                               scalar2=None, op0=ALU.mult), reads=[gq], writes=[gq])
        if e_ == 0:
            P.op("dve", lambda e: e.memset(oml[:], 1.0), writes=[oml])
        else:
            lbr = P.sb("lbr", [128, 2, 2, 8], F32)
            P.dma("sp", lbr[:], self.hg_lb_col.rearrange("d e p k -> p d e k"), lbr, writes=[lbr])
            for d_ in range(2):
                P.op("dve", lambda e, d_=d_: e.tensor_tensor(out=oml[:, d_ * 8:(d_ + 1) * 8], in0=lbr[:, d_, 0, :],
                                                             in1=lbr[:, d_, 1, :], op=ALU.subtract),
                     reads=[lbr], writes=[oml])
            P.op("act", lambda e: e.activation(out=oml[:], in_=oml[:], func=AF.Sigmoid), reads=[oml], writes=[oml])
        cn = {"w": 0, "st": 0, "sb": 0, "tmp": 0, "x": 0}

        def nxt(lst, key):
            b_ = lst[cn[key] % len(lst)]
            cn[key] += 1
            return b_

        def rms_bcast(pn, n):
            P.op("act", lambda e: e.activation(out=rs[:], in_=pn[:], func=AF.Sqrt, bias=self.epsc[:, 0:1],
                                               scale=1.0 / n), reads=[pn, self.epsc], writes=[rs])
            P.op("dve", lambda e: e.reciprocal(out=rs[:], in_=rs[:]), reads=[rs], writes=[rs])

        def norm512(src, g0, dst):
            P.op("act", lambda e: e.activation(out=sq4[:], in_=src[:], func=AF.Square), reads=[src], writes=[sq4])
            pn = self.nps()
            for c in range(4):
                P.op("pe", lambda e, c=c: e.matmul(pn[:], lhsT=self.ones[:], rhs=sq4[:, c, :], start=(c == 0),
                                                   stop=(c == 3)), reads=[self.ones, sq4], writes=[pn])
            rms_bcast(pn, 512)
            for c in range(4):
                P.op("dve", lambda e, c=c: e.scalar_tensor_tensor(
                    out=dst[:, c, :], in0=src[:, c, :], scalar=gq[:, g0 + c:g0 + c + 1], in1=rs[:],
                    op0=ALU.mult, op1=ALU.mult), reads=[src, gq, rs], writes=[dst])

        def rope_apply(xr, out):
            pw = self.nps()
            P.op("pe", lambda e: e.matmul(pw[0:64, :], lhsT=pswap[:, :], rhs=xr[0:64, :], start=True, stop=True),
                 reads=[pswap, xr], writes=[pw])
            t2 = nxt(tmp, "tmp")
            P.op("dve", lambda e: e.tensor_tensor(out=t2[0:64, :], in0=pw[0:64, :], in1=rope[:, 1, :], op=ALU.mult),
                 reads=[pw, rope], writes=[t2])
            P.op("pool", lambda e: e.tensor_tensor(out=out[0:64, :], in0=xr[0:64, :], in1=rope[:, 0, :], op=ALU.mult),
                 reads=[xr, rope], writes=[out])
            P.op("pool", lambda e: e.tensor_tensor(out=out[0:64, :], in0=out[0:64, :], in1=t2[0:64, :], op=ALU.add),
                 reads=[out, t2], writes=[out])

        cnt = 0
        for t0 in range(0, T, 512):
            self.load_norm_tile(l, t0, hT, xts, xns, stats, cnt)
            cnt += 4
            P.dma("sp", rope[:], self.rope_d[:, :, t0:t0 + 512].rearrange("a p t -> p a t"), rope, writes=[rope])
            P.dma("sp", mrow[:], self.tmask_row_d[0:1, t0:t0 + 512].partition_broadcast(128), mrow, writes=[mrow])
            for b in range(13):
                wb = nxt(wbl, "w")
                ncol = 512 if b < 12 else 64
                P.dma("sp", wb[:, :, 0:ncol], winb.t[:, b * 512:b * 512 + ncol].rearrange(
                    "(kc p) c -> p kc c", p=128), wb, writes=[wb], dram_r=[winb])
                if b < 2 or b in (8, 9):
                    for cc in range(4):
                        pst = self.nps()
                        self.fm_mm(pst, wb, cc * 128, 128, hT)
                        st = nxt(sts, "st")
                        P.op("act", lambda e, st=st, pst=pst: e.activation(out=st[:], in_=pst[:], func=AF.Silu),
                             reads=[pst], writes=[st])
                        row0 = (b * 512 if b < 2 else 5120 + (b - 8) * 512) + cc * 128
                        self.store_fm(st, row0, t0)
                elif b < 6:
                    d_ = (b - 2) // 2
                    for cc in range(4):
                        ch = ((b - 2) % 2) * 4 + cc
                        pst = self.nps()
                        self.fm_mm(pst, wb, cc * 128, 128, hT)
                        t1 = nxt(tmp, "tmp")
                        P.op("act", lambda e, t1=t1, pst=pst: e.activation(out=t1[:], in_=pst[:], func=AF.Sigmoid,
                                                                           scale=-1.0), reads=[pst], writes=[t1])
                        st = nxt(sts, "st")
                        P.op("dve", lambda e, st=st, t1=t1, d_=d_, ch=ch: e.scalar_tensor_tensor(
                            out=st[:], in0=t1[:], scalar=oml[:, d_ * 8 + ch:d_ * 8 + ch + 1], in1=mrow[:],
                            op0=ALU.mult, op1=ALU.mult), reads=[t1, oml, mrow], writes=[st])
                        self.store_fm(st, 1024 + d_ * 1024 + ch * 128, t0)
                        st2 = nxt(sts, "st")
                        P.op("act", lambda e, st=st, st2=st2: e.activation(out=st2[:], in_=st[:], func=AF.Ln, scale=-1.0,
                                                                           bias=self.onec[:, 0:1]),
                             reads=[st, self.onec], writes=[st2])
                        self.store_fm(st2, 3072 + d_ * 1024 + ch * 128, t0)
                elif b < 8:
                    for si in range(4):
                        pst = self.nps()
                        self.tm_mm(pst, wb, si, hT)
                        st = nxt(sts, "st")
                        ti = (t0 // 128) + si
                        P.op("act", lambda e, st=st, pst=pst, ti=ti: e.activation(
                            out=st[:], in_=pst[:], func=AF.Copy, scale=self.tmask[:, ti:ti + 1]),
                            reads=[pst, self.tmask], writes=[st])
                        P.dma("pool", self.V.t[t0 + si * 128:t0 + (si + 1) * 128, (b - 6) * 512:(b - 5) * 512],
                              st[:], st, reads=[st], dram_w=[self.V])
                elif b < 12:
                    dst = cq if b == 10 else ckv
                    for cc in range(4):
                        pst = self.nps()
                        self.fm_mm(pst, wb, cc * 128, 128, hT)
                        P.op("dve", lambda e, pst=pst, cc=cc, dst=dst: e.tensor_copy(out=dst[:, cc, :], in_=pst[:]),
                             reads=[pst], writes=[dst])
                else:
                    pst = self.nps()
                    self.fm_mm(pst, wb, 0, 64, hT)
                    P.op("dve", lambda e, pst=pst: e.tensor_copy(out=kpe[:], in_=pst[0:64, :]), reads=[pst], writes=[kpe])
            import os
            kev = os.environ.get("K_EV", "")
            if "nomla" in kev:
                continue
            norm512(cq, 0, cqn)
            norm512(ckv, 4, ckvn)
            P.op("act", lambda e: e.activation(out=sqk[0:64, :], in_=kpe[:], func=AF.Square), reads=[kpe], writes=[sqk])
            pk2 = self.nps()
            sqk_src = sqk if "dbgsq" not in kev else sq4
            P.op("pe", lambda e, pk2=pk2, sqk_src=sqk_src: e.matmul(pk2[:], lhsT=self.ones[:], rhs=sqk_src[:, 0:512] if sqk_src is sqk else sqk_src[:, 0, :], start=True, stop=True),
                 reads=[self.ones, sqk_src], writes=[pk2])
            P.op("act", lambda e, pk2=pk2: e.activation(out=skp[:], in_=pk2[:], func=AF.Copy), reads=[pk2], writes=[skp])
            kg = nxt(tmp, "tmp")
            P.op("dve", lambda e, kg=kg: e.tensor_scalar(out=kg[0:64, :], in0=kpe[:], scalar1=gq[0:64, 11:12],
                                                         scalar2=None, op0=ALU.mult), reads=[kpe, gq], writes=[kg])
            rope_apply(kg, krot)
            for h in range(int(os.environ.get("K_NH", "8")) if "noqk" not in kev else 0):
                for which in range(2):
                    if ("noq" in kev and which == 0) or ("nok" in kev and which == 1):
                        continue
                    w_ = wuq if which == 0 else wukv
                    src = cqn if which == 0 else ckvn
                    c0 = h * 192 if which == 0 else h * 256
                    gcol = 8 if which == 0 else 10
                    pq = self.nps()
                    for c in range(4):
                        P.op("pe", lambda e, c=c, w_=w_, src=src, c0=c0, pq=pq: e.matmul(
                            pq[:], lhsT=w_[:, c, c0:c0 + 128], rhs=src[:, c, :], start=(c == 0), stop=(c == 3)),
                            reads=[w_, src], writes=[pq])
                    if int(os.environ.get("K_STOP", "9")) <= 0:
                        continue
                    nf = nxt(tmp, "tmp")
                    sqn = nxt(tmp, "tmp")
                    P.op("dve", lambda e, nf=nf, pq=pq: e.tensor_copy(out=nf[:], in_=pq[:]), reads=[pq], writes=[nf])
                    P.op("act", lambda e, sqn=sqn, nf=nf: e.activation(out=sqn[:], in_=nf[:], func=AF.Square),
                         reads=[nf], writes=[sqn])
                    kstop = int(os.environ.get("K_STOP", "9"))
                    if kstop <= 1:
                        continue
                    if which == 0:
                        pr = self.nps()
                        for c in range(4):
                            P.op("pe", lambda e, c=c, c0=c0, pr=pr: e.matmul(
                                pr[0:64, :], lhsT=wuq[:, c, c0 + 128:c0 + 192], rhs=cqn[:, c, :], start=(c == 0),
                                stop=(c == 3)), reads=[wuq, cqn], writes=[pr])
                        qr = nxt(sts, "st")
                        sqr = sqrq
                        P.op("dve", lambda e, qr=qr, pr=pr: e.tensor_copy(out=qr[0:64, :], in_=pr[0:64, :]),
                             reads=[pr], writes=[qr])
                        P.op("act", lambda e, sqr=sqr, qr=qr: e.activation(out=sqr[0:64, :], in_=qr[0:64, :],
                                                                           func=AF.Square), reads=[qr], writes=[sqr])
                        P.op("dve", lambda e, qr=qr: e.tensor_scalar(
                            out=qr[0:64, :], in0=qr[0:64, :], scalar1=gq[0:64, 9:10], scalar2=None, op0=ALU.mult),
                            reads=[qr, gq], writes=[qr])
                        sq_r = sqr
                    else:
                        sq_r = sqk
                    pn = self.nps()
                    P.op("pe", lambda e, pn=pn, sqn=sqn: e.matmul(pn[:], lhsT=self.ones[:], rhs=sqn[:], start=True,
                                                                  stop=True), reads=[self.ones, sqn], writes=[pn])
                    if which == 0:
                        pn2 = self.nps()
                        P.op("pe", lambda e, pn2=pn2: e.matmul(pn2[:], lhsT=self.ones[:], rhs=sqrq[:], start=True,
                                                               stop=True), reads=[self.ones, sqrq], writes=[pn2])
                        add_sb = nxt(sts, "st")
                        P.op("act", lambda e, pn2=pn2, add_sb=add_sb: e.activation(out=add_sb[:], in_=pn2[:],
                                                                                   func=AF.Copy),
                             reads=[pn2], writes=[add_sb])
                    else:
                        add_sb = skp
                    ssum = nxt(sts, "st")
                    P.op("dve", lambda e, pn=pn, add_sb=add_sb, ssum=ssum: e.tensor_tensor(
                        out=ssum[:], in0=pn[:], in1=add_sb[:], op=ALU.add), reads=[pn, add_sb], writes=[ssum])
                    if "noadd" not in kev:
                        pn = ssum
                    if kstop <= 2:
                        continue
                    rms_bcast(pn, 192)
                    if kstop <= 3:
                        continue
                    ob = nxt(sbs, "sb")
                    P.op("dve", lambda e, ob=ob, nf=nf, gcol=gcol: e.scalar_tensor_tensor(
                        out=ob[:], in0=nf[:], scalar=gq[:, gcol:gcol + 1], in1=rs[:], op0=ALU.mult, op1=ALU.mult),
                        reads=[nf, gq, rs], writes=[ob])
                    dn = self.QN if which == 0 else self.KN
                    if "nodma" not in kev:
                        P.dma("pool", dn.t[h, :, t0:t0 + 512], ob[:], ob, reads=[ob], dram_w=[dn])
                    if kstop <= 4:
                        continue
                    ob2 = nxt(sbs, "sb")
                    if which == 0:
                        qrot = nxt(tmp, "tmp")
                        rope_apply(qr, qrot)
                        rsrc = qrot
                    else:
                        rsrc = krot
                    P.op("dve", lambda e, ob2=ob2, rsrc=rsrc: e.tensor_tensor(
                        out=ob2[0:64, :], in0=rsrc[0:64, :], in1=rs[0:64, :], op=ALU.mult),
                        reads=[rsrc, rs], writes=[ob2])
                    dr = self.QR if which == 0 else self.KR
                    if "nodma" not in kev:
                        P.dma("pool", dr.t[h, :, t0:t0 + 512], ob2[0:64, :], ob2, reads=[ob2], dram_w=[dr])
            wv = wukv[:].rearrange("p c (h x) -> p c h x", x=256)
            for si in range(4 if "nov" not in kev else 0):
                for half in range(2):
                    pv = self.nps()
                    for c in range(4):
                        P.op("pe", lambda e, c=c, si=si, half=half, pv=pv: e.matmul(
                            pv[:].rearrange("p (h x) -> p h x", x=128), lhsT=ckvn[:, c, si * 128:(si + 1) * 128],
                            rhs=wv[:, c, half * 4:(half + 1) * 4, 128:256], start=(c == 0), stop=(c == 3)),
                            reads=[ckvn, wukv], writes=[pv])
                    ob = nxt(sbs, "sb")
                    P.op("act", lambda e, ob=ob, pv=pv: e.activation(out=ob[:], in_=pv[:], func=AF.Copy),
                         reads=[pv], writes=[ob])
                    P.dma("pool", self.VA.t[t0 + si * 128:t0 + (si + 1) * 128, half * 512:(half + 1) * 512], ob[:], ob,
                          reads=[ob], dram_w=[self.VA])
        P.phase_reset()

    def phase_attn(self, l):
        P = self.P
        T = self.T
        NB = T // 128
        for db in (self.QN, self.QR, self.KN, self.KR, self.VA, self.MIXT):
            db.fence()
        onesb = P.sb("onesb", [128, 128], BF16)
        P.op("dve", lambda e: e.memset(onesb[:], 1.0), writes=[onesb])
        kn = [P.sb("kn%d" % i, [128, T], BF16) for i in range(2)]
        kr = [P.sb("kr%d" % i, [128, T], BF16) for i in range(2)]
        va = [P.sb("va%d" % i, [128, NB, 128], BF16) for i in range(2)]
        qn = [P.sb("qn%d" % i, [128, 512], BF16) for i in range(2)]
        qr = [P.sb("qr%d" % i, [128, 512], BF16) for i in range(2)]
        for i in range(2):
            P.op("pool", lambda e, i=i: e.memset(kr[i][64:128, :], 0.0), writes=[kr[i]])
            P.op("pool", lambda e, i=i: e.memset(qr[i][64:128, :], 0.0), writes=[qr[i]])
        pts = [P.sb("pt%d" % i, [128, 512], BF16) for i in range(3)]
        rd = [P.sb("rd%d" % i, [128, 512], F32) for i in range(2)]
        mx = [P.sb("mx%d" % i, [128, 512], BF16) for i in range(2)]
        nq = 0
        npt = 0
        for h in range(8):
            K1, K2, V1 = kn[h % 2], kr[h % 2], va[h % 2]
            P.dma("sp", K1[:], self.KN.t[h], K1, writes=[K1], dram_r=[self.KN])
            P.dma("sp", K2[0:64, :], self.KR.t[h], K2, writes=[K2], dram_r=[self.KR])
            P.dma("sp", V1[:], self.VA.t[:, h * 128:(h + 1) * 128].rearrange("(b p) d -> p b d", p=128), V1,
                  writes=[V1], dram_r=[self.VA])
            for t0 in range(0, T, 512):
                Q1, Q2 = qn[nq % 2], qr[nq % 2]
                po, pd = self.ps[(nq % 2) * 2], self.ps[(nq % 2) * 2 + 1]
                rdt, mxt = rd[nq % 2], mx[nq % 2]
                nq += 1
                P.dma("sp", Q1[:], self.QN.t[h, :, t0:t0 + 512], Q1, writes=[Q1], dram_r=[self.QN])
                P.dma("sp", Q2[0:64, :], self.QR.t[h, :, t0:t0 + 512], Q2, writes=[Q2], dram_r=[self.QR])
                for jb in range(NB):
                    psc = self.ps[4 + npt % 4]
                    pt = pts[npt % 3]
                    npt += 1
                    js = slice(jb * 128, (jb + 1) * 128)
                    P.op("pe", lambda e, psc=psc, js=js, K1=K1, Q1=Q1: e.matmul(
                        psc[:], lhsT=K1[:, js], rhs=Q1[:], start=True, stop=False), reads=[K1, Q1], writes=[psc])
                    P.op("pe", lambda e, psc=psc, js=js, K2=K2, Q2=Q2: e.matmul(
                        psc[:], lhsT=K2[:, js], rhs=Q2[:, :], start=False, stop=True), reads=[K2, Q2], writes=[psc])
                    P.op("act", lambda e, psc=psc, pt=pt, jb=jb: e.activation(
                        out=pt[:], in_=psc[:], func=AF.Exp, bias=self.kbias[:, jb:jb + 1]),
                        reads=[psc, self.kbias], writes=[pt])
                    P.op("pe", lambda e, po=po, pt=pt, jb=jb, V1=V1: e.matmul(
                        po[:], lhsT=V1[:, jb, :], rhs=pt[:], start=(jb == 0), stop=(jb == NB - 1)),
                        reads=[V1, pt], writes=[po])
                    P.op("pe", lambda e, pd=pd, pt=pt, jb=jb: e.matmul(
                        pd[:], lhsT=onesb[:], rhs=pt[:], start=(jb == 0), stop=(jb == NB - 1)),
                        reads=[onesb, pt], writes=[pd])
                P.op("dve", lambda e, rdt=rdt, pd=pd: e.reciprocal(out=rdt[:], in_=pd[:]), reads=[pd], writes=[rdt])
                P.op("dve", lambda e, rdt=rdt, po=po, mxt=mxt: e.tensor_tensor(out=mxt[:], in0=po[:], in1=rdt[:],
                                                                               op=ALU.mult),
                     reads=[po, rdt], writes=[mxt])
                P.dma("pool", self.MIXT.t[1024 + h * 128:1024 + (h + 1) * 128, t0:t0 + 512], mxt[:], mxt, reads=[mxt],
                      dram_w=[self.MIXT])
        P.phase_reset()


def col_layout(v):
    n = v.shape[-1] // 128
    return np.ascontiguousarray(np.swapaxes(v.reshape(v.shape[:-1] + (n, 128)), -1, -2))


def _const_masks():
    j = np.arange(128)[:, None]
    i = np.arange(128)[None, :]
    t = np.arange(128)[None, :] + 0 * j
    out = np.zeros((6, 128, 128), np.float32)
    out[0] = (j <= i)
    out[1] = (j >= i)
    out[2] = (t % 128 != 0)
    out[3] = (j <= i) & (j // 32 == i // 32)
    out[4] = (j >= i) & (j // 32 == i // 32)
    out[5] = (t % 32 != 0)
    return out


CONST_MASKS = _const_masks()
PSWAP = np.zeros((64, 64), np.float32)
for _m in range(64):
    PSWAP[(_m + 32) % 64, _m] = 1.0


def rope_tables(T):
    half = 32
    inv_freq = (10000.0 ** (-np.arange(half, dtype=np.float32) / half)).astype(np.float32)
    ang = np.arange(T, dtype=np.float32)[None, :] * inv_freq[:, None]
    cos, sin = np.cos(ang).astype(np.float32), np.sin(ang).astype(np.float32)
    tab = np.zeros((2, 64, T), np.float32)
    tab[0, :32] = cos
    tab[0, 32:] = cos
    tab[1, :32] = -sin
    tab[1, 32:] = sin
    return tab


def make_in_maps(inputs, T, ncores, layers=tuple(range(DEPTH))):
    xp, xs = inputs["x_prompt"], inputs["x_sample"]
    maps = []
    for c in range(ncores):
        if c < 4:
            x = xp[c]
            cc = inputs["c_prompt"][c]
        else:
            x = np.zeros((T, D), np.float32)
            x[: xs.shape[1]] = xs[c - 4]
            cc = inputs["c_sample"][c - 4]
        m = {
            "x": np.ascontiguousarray(x[:T]),
            "c_col": col_layout(cc),
            "ada_b_col": col_layout(inputs["ada_b"]),
            "norm_g_col": col_layout(inputs["norm_g"]).transpose(0, 2, 1, 3).reshape(DEPTH, 128, 3 * KC),
            "ident": np.eye(128, dtype=np.float32),
        }
        for l in layers:
            m["ada_w_%d" % l] = inputs["ada_w"][l]
            m["ffn_w13_%d" % l] = inputs["ffn_w13"][l]
            m["ffn_w2_%d" % l] = inputs["ffn_w2"][l]
        nreal = T if c < 4 else min(T, xs.shape[1])
        tm = (np.arange(T) < nreal).astype(np.float32)
        m["tmask_col"] = col_layout(tm)
        m["tmask_row"] = tm.reshape(1, T)
        m["kbias_col"] = col_layout(np.where(np.arange(T) < nreal, 0.0, -30000.0).astype(np.float32))
        m["masks"] = CONST_MASKS
        m["od_w_in"] = inputs["od_w_in"]
        m["od_w_out"] = inputs["od_w_out"]
        m["gla_gk_w2"] = inputs["gla_gk_w2"]
        m["gla_gk_b_col"] = col_layout(inputs["gla_gk_b"])
        m["gla_onorm_g_col"] = col_layout(inputs["gla_onorm_g"])
        m["ev_w_in"] = inputs["ev_w_in"]
        m["ev_w_out"] = inputs["ev_w_out"]
        m["hgrn_lb_col"] = col_layout(inputs["hgrn_lb"])
        m["hgrn_onorm_g_col"] = col_layout(inputs["hgrn_onorm_g"])
        m["mla_qa_norm_g_col"] = col_layout(inputs["mla_qa_norm_g"])
        m["mla_kva_norm_g_col"] = col_layout(inputs["mla_kva_norm_g"])
        m["mla_w_uq"] = inputs["mla_w_uq"]
        m["mla_w_ukv"] = inputs["mla_w_ukv"]
        m["mla_qn_g_col"] = col_layout(np.pad(inputs["mla_qn_g"], ((0, 0), (0, 64))))
        m["mla_kn_g_col"] = col_layout(np.pad(inputs["mla_kn_g"], ((0, 0), (0, 64))))
        m["rope_tab"] = rope_tables(T)
        m["pswap"] = PSWAP
        maps.append(m)
    return maps


def kernel(**inputs):
    inputs = {k: np.asarray(v) for k, v in inputs.items()}
    T = inputs["x_prompt"].shape[1]
    b = Builder(T, list(range(DEPTH)))
    maps = make_in_maps(inputs, T, 8)
    res = run_bass_kernel_spmd(b.nc, maps, core_ids=list(range(8)))
    yp = np.stack([res.results[c]["y"] for c in range(4)], 0)
    Ts = inputs["x_sample"].shape[1]
    ys = np.stack([res.results[c]["y"][:Ts] for c in range(4, 8)], 0)
    return (yp.astype(np.float32), ys.astype(np.float32))
```
